# Optimizing a Trainium2 kernel written in Bass

```python
import math
import jax, jax.numpy as jnp
from jax import lax
import numpy as np

D_MODEL = 1024
BATCH = 4
SEQ = 8192
DEPTH = 2

GRID_W = 64
CTX_LEN = 256
HEAD_DIM = 64
ROPE_BASE = 10000.0
EPS = 1e-6
NEG_INF = -1e30
Q_BLOCK = 128

NA_HEADS = 8
NA_WIN_R = 8
NA_WIN_C = 16
NA_QCB = 16
NA_KCB = 32
NA_WIDTH = NA_HEADS * HEAD_DIM
NA_SCALE = HEAD_DIM ** -0.5

MLA_HEADS = 8
MLA_Q_LORA = 384
MLA_KV_LORA = 256
MLA_NOPE = 64
MLA_ROPE = 32
MLA_V = 64
MLA_QK = MLA_NOPE + MLA_ROPE
MLA_WIDTH = MLA_HEADS * MLA_V
MLA_SCALE = MLA_QK ** -0.5

DIFF_HEADS = 4
DIFF_D = 64
DIFF_V = 2 * DIFF_D
DIFF_QK_WIDTH = DIFF_HEADS * 2 * DIFF_D
DIFF_WIDTH = DIFF_HEADS * DIFF_V
DIFF_SCALE = DIFF_D ** -0.5

N_BRANCH = 3
BRANCH_WIDTH = 512

IN_SPLITS = (NA_WIDTH, NA_WIDTH, NA_WIDTH,
             MLA_Q_LORA, MLA_KV_LORA, MLA_ROPE,
             DIFF_QK_WIDTH, DIFF_QK_WIDTH, DIFF_WIDTH,
             N_BRANCH * BRANCH_WIDTH, N_BRANCH * D_MODEL)
D_IN = sum(IN_SPLITS)

kernel_name = "hybrid_natten_mla_diffattn_prefix_dit"


def rms_norm(x, g):
    xf = x.astype(jnp.float32)
    y = xf * lax.rsqrt(jnp.mean(xf * xf, axis=-1, keepdims=True) + EPS)
    return (y * g.astype(jnp.float32)).astype(x.dtype)


def axial_rope_tables(n_tokens, rot_dim):
    t = jnp.arange(n_tokens, dtype=jnp.int32)
    row = (t // GRID_W).astype(jnp.float32)
    col = (t % GRID_W).astype(jnp.float32)
    n_freq = rot_dim // 4
    inv = ROPE_BASE ** (-jnp.arange(n_freq, dtype=jnp.float32) / n_freq)
    ang = jnp.concatenate([row[:, None] * inv, col[:, None] * inv], axis=-1)
    return jnp.cos(ang), jnp.sin(ang)


def apply_rope(x, cos, sin):
    half = x.shape[-1] // 2
    shape = (cos.shape[0],) + (1,) * (x.ndim - 3) + (half,)
    cs = cos.reshape(shape).astype(x.dtype)
    sn = sin.reshape(shape).astype(x.dtype)
    x1, x2 = x[..., :half], x[..., half:]
    return jnp.concatenate([x1 * cs - x2 * sn, x2 * cs + x1 * sn], axis=-1)


def rope_tail(x, cos, sin, n_rot):
    return jnp.concatenate([x[..., :-n_rot], apply_rope(x[..., -n_rot:], cos, sin)], axis=-1)


def softmax_attend(q, k, v, scale):
    s = jnp.einsum('bqhd,bkhd->bhqk', q, k).astype(jnp.float32) * scale
    p = jax.nn.softmax(s, axis=-1).astype(v.dtype)
    return jnp.einsum('bhqk,bkhv->bqhv', p, v)


def diff_attend(q, k, v, lam, scale):
    s = jnp.einsum('bqhmd,bkhmd->bhmqk', q, k).astype(jnp.float32) * scale
    p = jax.nn.softmax(s, axis=-1)
    a = (p[:, :, 0] - lam * p[:, :, 1]).astype(v.dtype)
    return jnp.einsum('bhqk,bkhv->bqhv', a, v)


def blocked_queries(fn, q):
    B, T = q.shape[0], q.shape[1]
    nb = T // Q_BLOCK
    qb = jnp.moveaxis(q.reshape((B, nb, Q_BLOCK) + q.shape[2:]), 1, 0)
    out = lax.map(fn, qb)
    out = jnp.moveaxis(out, 0, 1)
    return out.reshape((B, T) + out.shape[3:])


def neighbourhood_attend(q, k, v, k_ctx, v_ctx, rpb):
    B, T, H, d = q.shape
    L = k_ctx.shape[1]
    rows = T // GRID_W
    kr = min(NA_WIN_R, rows)
    ncb = GRID_W // NA_QCB
    qcol = np.arange(GRID_W).reshape(ncb, NA_QCB)
    cstart = np.clip(qcol - NA_WIN_C // 2, 0, GRID_W - NA_WIN_C)
    band0 = np.clip(np.arange(ncb) * NA_QCB - NA_WIN_C // 2, 0, GRID_W - NA_KCB)
    kcol = band0[:, None] + np.arange(NA_KCB)
    kcol_f = np.tile(kcol, (1, kr))
    krow = np.repeat(np.arange(kr), NA_KCB)
    col_in = jnp.asarray((kcol_f[:, None, :] >= cstart[:, :, None])
                         & (kcol_f[:, None, :] < cstart[:, :, None] + NA_WIN_C))
    dc_idx = np.clip(kcol_f[:, None, :] - qcol[:, :, None], -(NA_WIN_C - 1), NA_WIN_C - 1) + (NA_WIN_C - 1)
    nk = kr * NA_KCB
    qg = q.reshape(B, rows, ncb, NA_QCB, H, d)
    kg = k.reshape(B, rows, GRID_W, H, d)
    vg = v.reshape(B, rows, GRID_W, H, d)

    def row_fn(r):
        rs = jnp.clip(r - kr // 2, 0, rows - kr)
        kb = lax.dynamic_slice_in_dim(kg, rs, kr, axis=1)[:, :, kcol]
        vb = lax.dynamic_slice_in_dim(vg, rs, kr, axis=1)[:, :, kcol]
        kb = kb.transpose(0, 2, 1, 3, 4, 5).reshape(B, ncb, nk, H, d)
        vb = vb.transpose(0, 2, 1, 3, 4, 5).reshape(B, ncb, nk, H, d)
        qr = lax.dynamic_index_in_dim(qg, r, axis=1, keepdims=False)
        dr = rs + krow - r + (NA_WIN_R - 1)
        bias = rpb[:, dr[None, None, :], dc_idx].astype(jnp.float32)
        s_lat = jnp.einsum('bnqhd,bnkhd->bhnqk', qr, kb).astype(jnp.float32) * NA_SCALE + bias
        s_lat = jnp.where(col_in, s_lat, NEG_INF)
        s_ctx = jnp.einsum('bnqhd,bkhd->bhnqk', qr, k_ctx).astype(jnp.float32) * NA_SCALE
        p = jax.nn.softmax(jnp.concatenate([s_ctx, s_lat], axis=-1), axis=-1).astype(v.dtype)
        return (jnp.einsum('bhnqk,bkhd->bnqhd', p[..., :L], v_ctx)
                + jnp.einsum('bhnqk,bnkhd->bnqhd', p[..., L:], vb))

    out = lax.map(row_fn, jnp.arange(rows))
    return out.transpose(1, 0, 2, 3, 4, 5).reshape(B, T, H * d)


def mixer_inputs(h, w_in_l, na_q_g, na_k_g, mla_cq_g, mla_ckv_g, w_uq_l, w_ukv_l,
                 mla_q_g, mla_k_g, diff_q_g, diff_k_g):
    lead = h.shape[:-1]
    split_points = [int(s) for s in np.cumsum(IN_SPLITS)[:-1]]
    (na_q, na_k, na_v, cq, ckv, k_rope, dq, dk, dv, z, gm) = jnp.split(h @ w_in_l, split_points, axis=-1)
    na_q = rms_norm(na_q.reshape(lead + (NA_HEADS, HEAD_DIM)), na_q_g)
    na_k = rms_norm(na_k.reshape(lead + (NA_HEADS, HEAD_DIM)), na_k_g)
    na_v = na_v.reshape(lead + (NA_HEADS, HEAD_DIM))
    mq = (rms_norm(cq, mla_cq_g) @ w_uq_l).reshape(lead + (MLA_HEADS, MLA_QK))
    kv = (rms_norm(ckv, mla_ckv_g) @ w_ukv_l).reshape(lead + (MLA_HEADS, MLA_NOPE + MLA_V))
    k_nope, mv = kv[..., :MLA_NOPE], kv[..., MLA_NOPE:]
    k_r = jnp.broadcast_to(k_rope[..., None, :], lead + (MLA_HEADS, MLA_ROPE))
    mk = rms_norm(jnp.concatenate([k_nope, k_r], axis=-1), mla_k_g)
    mq = rms_norm(mq, mla_q_g)
    dq = rms_norm(dq.reshape(lead + (DIFF_HEADS, 2, DIFF_D)), diff_q_g)
    dk = rms_norm(dk.reshape(lead + (DIFF_HEADS, 2, DIFF_D)), diff_k_g)
    dv = dv.reshape(lead + (DIFF_HEADS, DIFF_V))
    return (na_q, na_k, na_v, mq, mk, mv, dq, dk, dv, z, gm)


def diff_finish(o, subln_g, lam_init):
    o = rms_norm(o, subln_g) * (1.0 - lam_init)
    return o.reshape(o.shape[:-2] + (DIFF_WIDTH,))


def merge_branches(o_na, o_mla, o_diff, z, gm, w_br_l, w_out_l):
    z_na, z_mla, z_diff = jnp.split(z, N_BRANCH, axis=-1)
    g_na, g_mla, g_diff = jnp.split(gm, N_BRANCH, axis=-1)
    y = (jax.nn.sigmoid(g_na) * ((o_na * jax.nn.silu(z_na)) @ w_br_l[0])
         + jax.nn.sigmoid(g_mla) * ((o_mla * jax.nn.silu(z_mla)) @ w_br_l[1])
         + jax.nn.sigmoid(g_diff) * ((o_diff * jax.nn.silu(z_diff)) @ w_br_l[2]))
    return y @ w_out_l


def setup_inputs(seed: int = 0) -> dict:
    key = jax.random.key(seed)
    ks = jax.random.split(key, 26)
    f32 = jnp.float32

    def nrm(k, shape, s):
        return jax.random.normal(k, shape, f32) * s

    def gain(k, shape):
        return 1.0 + 0.05 * jax.random.normal(k, shape, f32)

    D = D_MODEL
    return {
        'x': nrm(ks[0], (BATCH, SEQ, D), 1.0),
        'c': nrm(ks[1], (BATCH, D), 1.0),
        'ctx': nrm(ks[2], (BATCH, CTX_LEN, D), 1.0),
        'c_ctx': nrm(ks[3], (D,), 1.0),
        'norm_g': gain(ks[4], (DEPTH, D)),
        'w_ada': nrm(ks[5], (DEPTH, D, 3 * D), D ** -0.5),
        'b_ada': nrm(ks[6], (DEPTH, 3 * D), 0.02),
        'w_in': nrm(ks[7], (DEPTH, D, D_IN), D ** -0.5),
        'na_rpb': nrm(ks[8], (DEPTH, NA_HEADS, 2 * NA_WIN_R - 1, 2 * NA_WIN_C - 1), 0.1),
        'na_q_g': gain(ks[9], (DEPTH, HEAD_DIM)),
        'na_k_g': gain(ks[10], (DEPTH, HEAD_DIM)),
        'mla_cq_g': gain(ks[11], (DEPTH, MLA_Q_LORA)),
        'mla_ckv_g': gain(ks[12], (DEPTH, MLA_KV_LORA)),
        'w_uq': nrm(ks[13], (DEPTH, MLA_Q_LORA, MLA_HEADS * MLA_QK), MLA_Q_LORA ** -0.5),
        'w_ukv': nrm(ks[14], (DEPTH, MLA_KV_LORA, MLA_HEADS * (MLA_NOPE + MLA_V)), MLA_KV_LORA ** -0.5),
        'mla_q_g': gain(ks[15], (DEPTH, MLA_QK)),
        'mla_k_g': gain(ks[16], (DEPTH, MLA_QK)),
        'diff_q_g': gain(ks[17], (DEPTH, DIFF_D)),
        'diff_k_g': gain(ks[18], (DEPTH, DIFF_D)),
        'diff_lq1': nrm(ks[19], (DEPTH, DIFF_D), 0.1),
        'diff_lk1': nrm(ks[20], (DEPTH, DIFF_D), 0.1),
        'diff_lq2': nrm(ks[21], (DEPTH, DIFF_D), 0.1),
        'diff_lk2': nrm(ks[22], (DEPTH, DIFF_D), 0.1),
        'diff_subln_g': gain(ks[23], (DEPTH, DIFF_V)),
        'w_br': nrm(ks[24], (DEPTH, N_BRANCH, BRANCH_WIDTH, D), BRANCH_WIDTH ** -0.5),
        'w_out': nrm(ks[25], (DEPTH, D, D), D ** -0.5),
    }


def reference(x, c, ctx, c_ctx, norm_g, w_ada, b_ada, w_in, na_rpb, na_q_g, na_k_g,
              mla_cq_g, mla_ckv_g, w_uq, w_ukv, mla_q_g, mla_k_g, diff_q_g, diff_k_g,
              diff_lq1, diff_lk1, diff_lq2, diff_lk2, diff_subln_g, w_br, w_out):
    B, T, _ = x.shape
    f32 = jnp.float32
    cos_m, sin_m = axial_rope_tables(T, MLA_ROPE)
    cos_d, sin_d = axial_rope_tables(T, DIFF_D)
    for l in range(DEPTH):
        last = l == DEPTH - 1
        lam_init = 0.8 - 0.6 * math.exp(-0.3 * l)
        lam = (jnp.exp(jnp.sum(diff_lq1[l].astype(f32) * diff_lk1[l].astype(f32)))
               - jnp.exp(jnp.sum(diff_lq2[l].astype(f32) * diff_lk2[l].astype(f32))) + lam_init)
        shift, scale, gate = jnp.split(jax.nn.silu(c) @ w_ada[l] + b_ada[l], 3, axis=-1)
        shift_c, scale_c, gate_c = jnp.split(jax.nn.silu(c_ctx) @ w_ada[l] + b_ada[l], 3, axis=-1)
        h = rms_norm(x, norm_g[l]) * (1.0 + scale[:, None]) + shift[:, None]
        hc = rms_norm(ctx, norm_g[l]) * (1.0 + scale_c) + shift_c
        params = (w_in[l], na_q_g[l], na_k_g[l], mla_cq_g[l], mla_ckv_g[l], w_uq[l], w_ukv[l],
                  mla_q_g[l], mla_k_g[l], diff_q_g[l], diff_k_g[l])
        (na_q, na_k, na_v, m_q, m_k, m_v, d_q, d_k, d_v, z, gm) = mixer_inputs(h, *params)
        (cna_q, cna_k, cna_v, cm_q, cm_k, cm_v, cd_q, cd_k, cd_v, cz, cgm) = mixer_inputs(hc, *params)
        m_q = rope_tail(m_q, cos_m, sin_m, MLA_ROPE)
        m_k = rope_tail(m_k, cos_m, sin_m, MLA_ROPE)
        d_q = apply_rope(d_q, cos_d, sin_d)
        d_k = apply_rope(d_k, cos_d, sin_d)
        m_k_all = jnp.concatenate([cm_k, m_k], axis=1)
        m_v_all = jnp.concatenate([cm_v, m_v], axis=1)
        d_k_all = jnp.concatenate([cd_k, d_k], axis=1)
        d_v_all = jnp.concatenate([cd_v, d_v], axis=1)
        o_na = neighbourhood_attend(na_q, na_k, na_v, cna_k, cna_v, na_rpb[l])
        o_mla = blocked_queries(lambda qb: softmax_attend(qb, m_k_all, m_v_all, MLA_SCALE), m_q)
        o_mla = o_mla.reshape(B, T, MLA_WIDTH)
        o_diff = blocked_queries(lambda qb: diff_attend(qb, d_k_all, d_v_all, lam, DIFF_SCALE), d_q)
        o_diff = diff_finish(o_diff, diff_subln_g[l], lam_init)
        x_new = x + gate[:, None] * merge_branches(o_na, o_mla, o_diff, z, gm, w_br[l], w_out[l])
        if not last:
            co_na = softmax_attend(cna_q, cna_k, cna_v, NA_SCALE).reshape(B, -1, NA_WIDTH)
            co_mla = softmax_attend(cm_q, cm_k, cm_v, MLA_SCALE).reshape(B, -1, MLA_WIDTH)
            co_diff = diff_finish(diff_attend(cd_q, cd_k, cd_v, lam, DIFF_SCALE), diff_subln_g[l], lam_init)
            ctx = ctx + gate_c * merge_branches(co_na, co_mla, co_diff, cz, cgm, w_br[l], w_out[l])
        x = x_new
    return x
```

```python
import math
from contextlib import ExitStack

import numpy as np
import concourse.bass as bass
import concourse.mybir as mybir
from concourse.bass_utils import run_bass_kernel_spmd

F32 = mybir.dt.float32
BF16 = mybir.dt.bfloat16
AF = mybir.ActivationFunctionType
ALU = mybir.AluOpType
AX = mybir.AxisListType

D_MODEL = 1024
BATCH = 4
SEQ = 8192
DEPTH = 2
GRID_W = 64
CTX = 256
EPS = 1e-6
D_IN = 8352
NA_SCALE = 64 ** -0.5
MLA_SCALE = 96 ** -0.5
DIFF_SCALE = 64 ** -0.5
HALF = SEQ // 2
NQ = CTX + HALF
NQ2 = CTX + SEQ
NK = CTX + SEQ
NKT = NK // 128
NLT = HALF // 128
NTBL = 5 + 24 + 24
MASKVAL = -240000.0

ENGS = ['pe', 'act', 'dve', 'pool', 'sp']
DEBUG_STOP = None
DEBUG_SCR = False
DBG0 = 99
DBG1 = 99
DBG1_TILES = 99
DBG2 = 99
DBG3 = 99
DBG3_TILES = 99
DBG3X = 99


class Op:
    __slots__ = ('eng', 'fn', 'deps', 'is_dma', 'key', 'signal', 'sig_idx', 'dma_val')

    def __init__(self, eng, fn, is_dma, key):
        self.eng = eng
        self.fn = fn
        self.deps = []
        self.is_dma = is_dma
        self.key = key
        self.signal = is_dma
        self.sig_idx = None
        self.dma_val = None


class Prog:
    def __init__(self, nc):
        self.nc = nc
        self.q = {e: [] for e in ENGS}
        self.last_w = {}
        self.readers = {}
        self.dma_count = {}
        self.dma_last = {}
        self.nops = 0

    def _dep(self, op, prod):
        if prod is None or prod is op:
            return
        if (not prod.is_dma) and (not op.is_dma) and prod.eng == op.eng and op.eng in ('pe', 'sp'):
            return
        if prod not in op.deps:
            op.deps.append(prod)
            prod.signal = True

    def add(self, eng, fn, reads=(), writes=(), dma=False, key=None):
        op = Op(eng, fn, dma, key)
        for r in reads:
            self._dep(op, self.last_w.get(r))
        for w in writes:
            self._dep(op, self.last_w.get(w))
            for rd in self.readers.get(w, ()):
                self._dep(op, rd)
        for r in reads:
            self.readers.setdefault(r, []).append(op)
        for w in writes:
            self.last_w[w] = op
            self.readers[w] = []
        if dma:
            prev = self.dma_last.get(key)
            if prev is not None:
                self._dep(op, prev)
            self.dma_last[key] = op
            n = self.dma_count.get(key, 0) + 1
            self.dma_count[key] = n
            op.dma_val = 16 * n
        self.q[eng].append(op)
        self.nops += 1
        return op

    def dma(self, eng, out, in_, reads, writes, key):
        return self.add(eng, lambda e: e.dma_start(out=out, in_=in_), reads, writes, dma=True, key=key)

    def barrier(self):
        lasts = [self.q[e][-1] for e in ENGS if self.q[e]]
        lasts = [p for p in lasts if p.fn is not None]
        dmas = list(self.dma_last.values())
        for e in ENGS:
            op = Op(e, None, False, None)
            for p in lasts + dmas:
                if p.is_dma or p.eng != e:
                    if p not in op.deps:
                        op.deps.append(p)
                        p.signal = True
            self.q[e].append(op)
        self.last_w = {}
        self.readers = {}

    def finish(self, res):
        op = Op('sp', None, False, None)
        for r in res:
            p = self.last_w.get(r)
            if p is not None and p not in op.deps:
                op.deps.append(p)
                p.signal = True
        self.q['sp'].append(op)

    def emit(self, stack):
        nc = self.nc
        for e in ENGS:
            c = 0
            for op in self.q[e]:
                if not op.is_dma and op.signal:
                    c += 1
                    op.sig_idx = c
        esem = {e: stack.enter_context(nc.semaphore('s_' + e)) for e in ENGS if e != 'sp'}
        dsem = {k: stack.enter_context(nc.semaphore('d%d' % i)) for i, k in enumerate(self.dma_count)}
        self.n_sems = len(esem) + len(dsem)
        block = stack.enter_context(nc.Block())
        q = self.q

        def run(e, handle):
            seen = {}
            for op in q[e]:
                for p in op.deps:
                    if p.is_dma:
                        s, v = dsem[p.key], p.dma_val
                    else:
                        s, v = esem[p.eng], p.sig_idx
                    sid = id(s)
                    if seen.get(sid, 0) >= v:
                        continue
                    seen[sid] = v
                    handle.wait_ge(s, v)
                if op.fn is None:
                    continue
                ins = op.fn(handle)
                if op.is_dma:
                    ins.then_inc(dsem[op.key], 16)
                elif op.signal:
                    ins.then_inc(esem[e], 1)

        @block.tensor
        def _(h):
            run('pe', h)

        @block.scalar
        def _(h):
            run('act', h)

        @block.vector
        def _(h):
            run('dve', h)

        @block.gpsimd
        def _(h):
            run('pool', h)

        @block.sync
        def _(h):
            run('sp', h)


class T:
    def __init__(self, ap, res):
        self.t = ap
        self.res = res

    def __getitem__(self, k):
        return self.t[k]


class Ring:
    def __init__(self, tiles):
        self.tiles = tiles
        self.i = 0

    def next(self):
        t = self.tiles[self.i % len(self.tiles)]
        self.i += 1
        return t


def declare_layer_weights(nc, sfx):
    w = {}

    def d(name, shape):
        w[name] = nc.dram_tensor(name + sfx, shape, F32, kind="ExternalInput").ap()
    d('norm_g', [1024]); d('w_ada', [1024, 3072]); d('b_ada', [3072]); d('w_in', [1024, D_IN])
    d('nab', [8, 128, NTBL, 128])
    d('na_q_g', [64]); d('na_k_g', [64]); d('mla_cq_g', [384]); d('mla_ckv_g', [256])
    d('w_uq', [384, 768]); d('w_ukv', [256, 1024]); d('mla_q_g', [96]); d('mla_k_g', [96])
    d('diff_q_g', [64]); d('diff_k_g', [64]); d('diff_l', [4, 64]); d('diff_subln_g', [128])
    d('w_br', [3, 512, 1024]); d('w_out', [1024, 1024])
    return w


class Ops:
    def __init__(self, P):
        self.P = P

    def mm(self, out, lhsT, rhs, start, stop, reads, writes, skip=False):
        self.P.add('pe', lambda e: e.matmul(out, lhsT=lhsT, rhs=rhs, start=start, stop=stop, skip_group_check=skip), reads, writes)

    def tr(self, out, in_, ident, reads, writes):
        self.P.add('pe', lambda e: e.transpose(out=out, in_=in_, identity=ident), reads, writes)

    def act(self, out, in_, func, reads, writes, **kw):
        self.P.add('act', lambda e: e.activation(out=out, in_=in_, func=func, **kw), reads, writes)

    def acopy(self, out, in_, reads, writes):
        self.P.add('act', lambda e: e.copy(out=out, in_=in_), reads, writes)

    def copy(self, eng, out, in_, reads, writes):
        self.P.add(eng, lambda e: e.tensor_copy(out=out, in_=in_), reads, writes)

    def tt(self, eng, out, in0, in1, op, reads, writes):
        self.P.add(eng, lambda e: e.tensor_tensor(out=out, in0=in0, in1=in1, op=op), reads, writes)

    def ts(self, eng, out, in0, s1, op0, reads, writes, s2=None, op1=None):
        if op1 is None:
            self.P.add(eng, lambda e: e.tensor_scalar(out=out, in0=in0, scalar1=s1, scalar2=None, op0=op0), reads, writes)
        else:
            self.P.add(eng, lambda e: e.tensor_scalar(out=out, in0=in0, scalar1=s1, scalar2=s2, op0=op0, op1=op1), reads, writes)

    def stt(self, eng, out, in0, scalar, in1, op0, op1, reads, writes):
        self.P.add(eng, lambda e: e.scalar_tensor_tensor(out=out, in0=in0, scalar=scalar, in1=in1, op0=op0, op1=op1), reads, writes)

    def red(self, eng, out, in_, reads, writes):
        self.P.add(eng, lambda e: e.tensor_reduce(out=out, in_=in_, axis=AX.X, op=ALU.add), reads, writes)

    def recip(self, out, in_, reads, writes):
        nc = self.P.nc

        def fn(e):
            with nc.allow_low_precision(reason="fp32 reciprocal rounded once to the bf16 consumer dtype"):
                return e.reciprocal(out=out, in_=in_)
        self.P.add('dve', fn, reads, writes)

    def memset(self, eng, ap, val, writes):
        self.P.add(eng, lambda e: e.memset(ap, val), [], writes)


def emit_layer(nc, P, l, last, W, C, xown, xother, ctxin, xout, ctxout, scr, both=False, xout_other=None):
    lam_init = 0.8 - 0.6 * math.exp(-0.3 * l)
    L = 'L%d_' % l
    NQL = NQ2 if both else NQ
    O = Ops(P)

    def sbt(stack, name, shape, dt):
        return T(stack.enter_context(nc.sbuf_tensor(L + name, shape, dt)), L + name)

    def pst(stack, name, shape, dt):
        return T(stack.enter_context(nc.psum_tensor(L + name, shape, dt)), L + name)

    def sring(stack, name, n, shape, dt):
        return Ring([sbt(stack, '%s%d' % (name, i), shape, dt) for i in range(n)])

    def pring(stack, name, n, shape, dt):
        return Ring([pst(stack, '%s%d' % (name, i), shape, dt) for i in range(n)])

    def v3(ap, h):
        return ap.rearrange("p (h d) -> p h d", h=h)

    with ExitStack() as LS:
        ident_f = sbt(LS, 'ident_f', [128, 128], F32)
        ident_b = sbt(LS, 'ident_b', [128, 128], BF16)
        ones_f = sbt(LS, 'ones_f', [1, 128], F32)
        eps_t = sbt(LS, 'eps_t', [128, 1], F32)
        A_ = [sbt(LS, 'A%d' % i, [128, 1024], F32) for i in range(2)]
        Sh_ = [sbt(LS, 'Sh%d' % i, [128, 1024], F32) for i in range(2)]
        Gt_ = [sbt(LS, 'Gt%d' % i, [128, 1024], F32) for i in range(2)]
        nlam_b = sbt(LS, 'nlam_b', [128, 1], F32)
        gains = {}
        for nm, n in (('na_q_g', 64), ('na_k_g', 64), ('mla_cq_g', 384), ('mla_ckv_g', 256), ('mla_q_g', 96),
                      ('mla_k_g', 96), ('diff_q_g', 64), ('diff_k_g', 64), ('diff_subln_g', 128)):
            gains[nm] = sbt(LS, nm, [128, n], F32)
            P.dma('sp', gains[nm][:], W[nm].partition_broadcast(128), [], [gains[nm].res], key='small')
        P.dma('sp', ident_f[:], C['ident'], [], [ident_f.res], key='small')
        O.copy('dve', ident_b[:], ident_f[:], [ident_f.res], [ident_b.res])
        O.memset('dve', ones_f[:], 1.0, [ones_f.res])
        O.memset('dve', eps_t[:], EPS, [eps_t.res])
        sg = gains['diff_subln_g']
        O.ts('dve', sg[:], sg[:], (1.0 - lam_init), ALU.mult, [sg.res], [sg.res])

        with ExitStack() as ph:
            wada = sbt(ph, 'wada', [128, 8, 3072], F32)
            wsrc = W['w_ada'].rearrange("(p k) n -> p k n", k=8)
            for i in range(4):
                P.dma('sp' if i % 2 == 0 else 'act', wada[:, 2 * i:2 * i + 2, :], wsrc[:, 2 * i:2 * i + 2, :], [],
                      [wada.res + str(i)], key='wada%d' % i)
            ccol = sbt(ph, 'ccol', [128, 2, 8], F32)
            P.dma('sp', ccol[:, 0, :], C['cvec'].rearrange("(p k) -> p k", k=8), [], [ccol.res + 'a'], key='small')
            P.dma('sp', ccol[:, 1, :], C['cctx'].rearrange("(p k) -> p k", k=8), [], [ccol.res + 'b'], key='small')
            sig = sbt(ph, 'sig', [128, 2, 8], F32)
            scol = sbt(ph, 'scol', [128, 2, 8], F32)
            O.act(sig[:], ccol[:], AF.Sigmoid, [ccol.res + 'a', ccol.res + 'b'], [sig.res])
            O.tt('dve', scol[:], ccol[:], sig[:], ALU.mult, [sig.res, ccol.res + 'a', ccol.res + 'b'], [scol.res])
            brow = sbt(ph, 'brow', [1, 3072], F32)
            P.dma('sp', brow[:], W['b_ada'].rearrange("(o n) -> o n", o=1), [], [brow.res], key='small')
            gnb = sbt(ph, 'gnb', [128, 1024], F32)
            P.dma('sp', gnb[:], W['norm_g'].partition_broadcast(128), [], [gnb.res], key='small')
            modrow = [sbt(ph, 'modrow%d' % i, [1, 3072], F32) for i in range(2)]
            pmod = pring(ph, 'pmod', 2, [128, 512], F32)
            for which in range(2 if DBG0 >= 2 else 0):
                for cg in range(6):
                    ps = pmod.next()
                    for k in range(8):
                        O.mm(ps[0:1, :], scol[:, which, k:k + 1], wada[:, k, cg * 512:(cg + 1) * 512], k == 0, k == 7,
                             [scol.res, wada.res + str(k // 2)], [ps.res])
                    O.tt('dve', modrow[which][0:1, cg * 512:(cg + 1) * 512], ps[0:1, :], brow[0:1, cg * 512:(cg + 1) * 512],
                         ALU.add, [ps.res, brow.res], [modrow[which].res])
            for which in range(2 if DBG0 >= 3 else 0):
                for cg in range(6):
                    ps = pmod.next()
                    O.mm(ps[:, :], ones_f[0:1, :], modrow[which][0:1, cg * 512:(cg + 1) * 512], True, True,
                         [ones_f.res, modrow[which].res], [ps.res])
                    c0 = (cg % 2) * 512
                    if cg < 2:
                        O.acopy(Sh_[which][:, c0:c0 + 512], ps[:, :], [ps.res], [Sh_[which].res])
                    elif cg < 4:
                        O.stt('dve', A_[which][:, c0:c0 + 512], ps[:, :], 1.0, gnb[:, c0:c0 + 512], ALU.add, ALU.mult,
                              [ps.res, gnb.res], [A_[which].res])
                    else:
                        O.acopy(Gt_[which][:, c0:c0 + 512], ps[:, :], [ps.res], [Gt_[which].res])
            if DBG0 < 4:
                P.barrier()
                return
            lrow = sbt(ph, 'lrow', [128, 4, 64], F32)
            P.dma('sp', lrow[:], W['diff_l'].rearrange("a d -> (a d)").partition_broadcast(128), [], [lrow.res], key='small')
            lprod = sbt(ph, 'lprod', [128, 2, 64], F32)
            lsum = sbt(ph, 'lsum', [128, 8], F32)
            O.tt('dve', lprod[:, 0, :], lrow[:, 0, :], lrow[:, 1, :], ALU.mult, [lrow.res], [lprod.res])
            O.tt('dve', lprod[:, 1, :], lrow[:, 2, :], lrow[:, 3, :], ALU.mult, [lrow.res, lprod.res], [lprod.res])
            O.red('dve', lsum[:, 0:2], lprod[:], [lprod.res], [lsum.res])
            O.act(lsum[:, 2:4], lsum[:, 0:2], AF.Exp, [lsum.res], [lsum.res])
            O.tt('dve', lsum[:, 4:5], lsum[:, 3:4], lsum[:, 2:3], ALU.subtract, [lsum.res], [lsum.res])
            O.ts('dve', nlam_b[:], lsum[:, 4:5], -lam_init, ALU.add, [lsum.res], [nlam_b.res])
        P.barrier()
        if DEBUG_STOP == 0 and DBG0 == 4:
            return
        if DEBUG_STOP == 0:
            for i, tl in enumerate([A_[0], Sh_[0], Gt_[0], A_[1], Sh_[1], Gt_[1]][:DBG0 - 4]):
                P.dma('sp', xout[i * 128:(i + 1) * 128, :], tl[:], [], [], key='dbg')
            P.barrier()
            return

        with ExitStack() as ph:
            NQKV = 3744
            wq = sbt(ph, 'wq', [128, 8, NQKV], BF16)
            wsrc = W['w_in'].rearrange("(k p) n -> p k n", p=128)
            for k in range(8):
                P.dma('pool', wq[:, k, :], wsrc[:, k, 0:NQKV], [], [wq.res + str(k)], key='wl%d' % k)
            wq_res = [wq.res + str(k) for k in range(8)]
            wuq = sbt(ph, 'wuq', [128, 3, 768], BF16)
            P.dma('pool', wuq[:], W['w_uq'].rearrange("(k p) n -> p k n", p=128), [], [wuq.res], key='wl0')
            wukv = sbt(ph, 'wukv', [128, 2, 1024], BF16)
            P.dma('pool', wukv[:], W['w_ukv'].rearrange("(k p) n -> p k n", p=128), [], [wukv.res], key='wl1')

            xt_r = sring(ph, 'xt', 2, [128, 1024], F32)
            rope_r = sring(ph, 'rope', 2, [128, 96], F32)
            junk = sbt(ph, 'junk', [128, 1024], F32)
            st_r = sring(ph, 'st', 2, [128, 4], F32)
            h1 = sbt(ph, 'h1', [128, 1024], F32)
            hb = sbt(ph, 'hb', [128, 1024], BF16)
            hT_r = sring(ph, 'hT', 2, [128, 8, 128], BF16)
            sq_r = sring(ph, 'sq', 2, [128, 768], F32)
            ss_r = sring(ph, 'ss', 3, [128, 24], F32)
            t_r = sring(ph, 't', 2, [128, 768], F32)
            tg_r = sring(ph, 'tg', 3, [128, 768], F32)
            dst_r = sring(ph, 'dst', 3, [128, 768], BF16)
            rtmp = [sbt(ph, 'rtmp%d' % i, [128, 256], F32) for i in range(4)]
            mkraw = sbt(ph, 'mkraw', [128, 8, 96], F32)
            krs = sbt(ph, 'krs', [128, 32], F32)
            cT_r = sring(ph, 'cT', 2, [128, 3, 128], BF16)
            stg4_r = sring(ph, 'stg4', 3, [128, 4, 128], BF16)
            stg8_r = sring(ph, 'stg8', 2, [96, 8, 128], BF16)
            vst_r = sring(ph, 'vst', 3, [128, 8, 66], BF16)
            vstd_r = sring(ph, 'vstd', 2, [128, 4, 132], BF16)
            for tl in vst_r.tiles + vstd_r.tiles:
                O.memset('pool', tl[:], 1.0, [tl.res])

            pT = pst(ph, 'pT', [128, 8, 128], BF16)
            pj_r = pring(ph, 'pj', 4, [128, 512], F32)
            ptq_r = pring(ph, 'ptq', 2, [128, 8, 128], BF16)

            def headnorm(src_ap, src_res, H, D, gain):
                n = H * D
                sq = sq_r.next(); ss = ss_r.next(); t = t_r.next(); tg = tg_r.next()
                O.act(sq[:, :n], src_ap, AF.Square, [src_res], [sq.res])
                O.red('dve', ss[:, 0:H], v3(sq[:, :n], H), [sq.res], [ss.res])
                O.act(ss[:, 8:8 + H], ss[:, 0:H], AF.Ln, [ss.res, eps_t.res], [ss.res], scale=1.0 / D, bias=eps_t[:])
                O.act(ss[:, 16:16 + H], ss[:, 8:8 + H], AF.Exp, [ss.res], [ss.res], scale=-0.5)
                O.tt('dve', v3(t[:, :n], H), v3(src_ap, H), ss[:, 16:16 + H].unsqueeze(2).to_broadcast([128, H, D]), ALU.mult,
                     [src_res, ss.res], [t.res])
                O.tt('pool', v3(tg[:, :n], H), v3(t[:, :n], H), gain[:, 0:D].unsqueeze(1).to_broadcast([128, H, D]), ALU.mult,
                     [t.res, gain.res], [tg.res])
                return tg

            def rope_into(t3, t_res, d3, d_res, H, r0, R, rope):
                if rope is None:
                    O.acopy(d3, t3, [t_res, d_res], [d_res])
                    return
                rt, co, so = rope
                hh = R // 2
                if r0 > 0:
                    O.acopy(d3[:, :, 0:r0], t3[:, :, 0:r0], [t_res, d_res], [d_res])
                x1 = t3[:, :, r0:r0 + hh]
                x2 = t3[:, :, r0 + hh:r0 + R]
                cs = rt[:, co:co + hh].unsqueeze(1).to_broadcast([128, H, hh])
                sn = rt[:, so:so + hh].unsqueeze(1).to_broadcast([128, H, hh])
                ra, rb, rc, rd = [v3(x[:, :H * hh], H) for x in rtmp]
                O.tt('dve', ra, x1, cs, ALU.mult, [t_res, rt.res], [rtmp[0].res])
                O.tt('pool', rb, x2, sn, ALU.mult, [t_res, rt.res], [rtmp[1].res])
                O.tt('pool', rc, x2, cs, ALU.mult, [t_res, rt.res], [rtmp[2].res])
                O.tt('dve', rd, x1, sn, ALU.mult, [t_res, rt.res], [rtmp[3].res])
                O.tt('dve', d3[:, :, r0:r0 + hh], ra, rb, ALU.subtract, [rtmp[0].res, rtmp[1].res, d_res], [d_res])
                O.tt('pool', d3[:, :, r0 + hh:r0 + R], rc, rd, ALU.add, [rtmp[2].res, rtmp[3].res, d_res], [d_res])

            def tstore4(dst, scr_ap, col0):
                pq = ptq_r.next()
                for c in range(4):
                    O.tr(pq[:, c, :], dst[:, c * 128:(c + 1) * 128], ident_b[:], [dst.res, ident_b.res], [pq.res])
                stg = stg4_r.next()
                O.acopy(stg[:], pq[:, 0:4, :], [pq.res], [stg.res])
                P.dma('sp', scr_ap.rearrange("(c p) n -> p c n", p=128)[:, :, col0:col0 + 128], stg[:], [stg.res], [], key=stg.res)

            def tstore8(dst, scr_ap, col0):
                pq = ptq_r.next()
                for c in range(8):
                    O.tr(pq[0:96, c, :], dst[:, c * 96:(c + 1) * 96], ident_b[:], [dst.res, ident_b.res], [pq.res])
                stg = stg8_r.next()
                O.acopy(stg[:], pq[0:96, :, :], [pq.res], [stg.res])
                P.dma('sp', scr_ap.rearrange("h p n -> p h n")[:, :, col0:col0 + 128], stg[:], [stg.res], [], key=stg.res)

            def proj(hT, c0, c1):
                ps = pj_r.next()
                n = c1 - c0
                for k in range(8):
                    O.mm(ps[:, 0:n], hT[:, k, :], wq[:, k, c0:c1], k == 0, k == 7, [hT.res, wq_res[k]], [ps.res])
                return ps

            def qk_simple(hT, c0, gain, rope, scr_ap, col0):
                ps = proj(hT, c0, c0 + 512)
                tg = headnorm(ps[:, :], ps.res, 8, 64, gain)
                dst = dst_r.next()
                rope_into(v3(tg[:, :512], 8), tg.res, v3(dst[:, :512], 8), dst.res, 8, 0, 64, rope)
                tstore4(dst, scr_ap, col0)

            def do_tile(src_ap, which, rope_src, full, kt, qt):
                xt = xt_r.next()
                P.dma('sp', xt[:], src_ap, [], [xt.res], key=xt.res)
                rp = None
                if rope_src is not None:
                    rp = rope_r.next()
                    P.dma('sp', rp[:], rope_src, [], [rp.res], key=rp.res)
                st = st_r.next()
                O.memset('dve', st[:], 0.0, [st.res])
                O.act(junk[:], xt[:], AF.Square, [xt.res, st.res], [junk.res, st.res], accum_out=st[:, 0:1])
                O.act(st[:, 1:2], st[:, 0:1], AF.Ln, [st.res, eps_t.res], [st.res], scale=1.0 / 1024, bias=eps_t[:])
                O.act(st[:, 2:3], st[:, 1:2], AF.Exp, [st.res], [st.res], scale=-0.5)
                O.stt('dve', h1[:], xt[:], st[:, 2:3], A_[which][:], ALU.mult, ALU.mult, [xt.res, st.res, A_[which].res], [h1.res])
                O.tt('pool', hb[:], h1[:], Sh_[which][:], ALU.add, [h1.res, Sh_[which].res], [hb.res])
                for k in range(8):
                    O.tr(pT[:, k, :], hb[:, k * 128:(k + 1) * 128], ident_b[:], [hb.res, ident_b.res], [pT.res])
                hT = hT_r.next()
                O.acopy(hT[:], pT[:], [pT.res], [hT.res])

                if DBG1 < 2:
                    return
                ropem = None if rp is None else (rp, 0, 16)
                roped = None if rp is None else (rp, 32, 64)
                if full:
                    qk_simple(hT, 0, gains['na_q_g'], None, scr['QT_na'], qt * 128)
                qk_simple(hT, 512, gains['na_k_g'], None, scr['KT_na'], kt * 128)
                ps = proj(hT, 1024, 1536)
                vs = vst_r.next()
                O.acopy(vs[:, :, 0:64], v3(ps[:, :], 8), [ps.res, vs.res], [vs.res])
                P.dma('sp', scr['V_na'].rearrange("h p t d -> p h t d")[:, :, kt, :], vs[:], [vs.res], [], key=vs.res)
                if DBG1 < 3:
                    return
                if full:
                    ps = proj(hT, 1536, 1920)
                    tg = headnorm(ps[:, 0:384], ps.res, 1, 384, gains['mla_cq_g'])
                    dst = dst_r.next()
                    O.acopy(dst[:, :384], tg[:, :384], [tg.res], [dst.res])
                    pq = ptq_r.next()
                    for c in range(3):
                        O.tr(pq[:, c, :], dst[:, c * 128:(c + 1) * 128], ident_b[:], [dst.res, ident_b.res], [pq.res])
                    cT = cT_r.next()
                    O.acopy(cT[:], pq[:, 0:3, :], [pq.res], [cT.res])
                    dstq = dst_r.next()
                    for g in range(2):
                        ps = pj_r.next()
                        for k in range(3):
                            O.mm(ps[:, 0:384], cT[:, k, :], wuq[:, k, g * 384:(g + 1) * 384], k == 0, k == 2, [cT.res, wuq.res], [ps.res])
                        tg = headnorm(ps[:, 0:384], ps.res, 4, 96, gains['mla_q_g'])
                        rope_into(v3(tg[:, :384], 4), tg.res, v3(dstq[:, g * 384:(g + 1) * 384], 4), dstq.res, 4, 64, 32, ropem)
                    tstore8(dstq, scr['QT_m'], qt * 128)
                if DBG1 < 4:
                    return
                pskv = proj(hT, 1920, 2208)
                tg = headnorm(pskv[:, 0:256], pskv.res, 1, 256, gains['mla_ckv_g'])
                dst = dst_r.next()
                O.acopy(dst[:, :256], tg[:, :256], [tg.res], [dst.res])
                pq = ptq_r.next()
                for c in range(2):
                    O.tr(pq[:, c, :], dst[:, c * 128:(c + 1) * 128], ident_b[:], [dst.res, ident_b.res], [pq.res])
                cT = cT_r.next()
                O.acopy(cT[:, 0:2, :], pq[:, 0:2, :], [pq.res], [cT.res])
                if DBG1 < 4.1:
                    return
                O.acopy(krs[:], pskv[:, 256:288], [pskv.res], [krs.res])
                O.copy('pool', mkraw[:, :, 64:96], krs[:].unsqueeze(1).to_broadcast([128, 8, 32]), [krs.res, mkraw.res], [mkraw.res])
                if DBG1 < 4.2:
                    return
                vs = vst_r.next()
                for g in range(2):
                    ps = pj_r.next()
                    for k in range(2):
                        O.mm(ps[:, :], cT[:, k, :], wukv[:, k, g * 512:(g + 1) * 512], k == 0, k == 1, [cT.res, wukv.res], [ps.res])
                    p3 = v3(ps[:, :], 4)
                    if DBG1 >= 4.22:
                        O.acopy(mkraw[:, 4 * g:4 * g + 4, 0:64], p3[:, :, 0:64], [ps.res, mkraw.res], [mkraw.res])
                    if DBG1 >= 4.24:
                        O.acopy(vs[:, 4 * g:4 * g + 4, 0:64], p3[:, :, 64:128], [ps.res, vs.res], [vs.res])
                if DBG1 >= 4.26:
                    P.dma('sp', scr['V_m'].rearrange("h p t d -> p h t d")[:, :, kt, :], vs[:], [vs.res], [], key=vs.res)
                if DBG1 < 4.3:
                    return
                tg = headnorm(mkraw[:].rearrange("p h d -> p (h d)"), mkraw.res, 8, 96, gains['mla_k_g'])
                if DBG1 < 4.4:
                    return
                dstk = dst_r.next()
                rope_into(v3(tg[:, :768], 8), tg.res, v3(dstk[:, :768], 8), dstk.res, 8, 64, 32, ropem)
                tstore8(dstk, scr['KT_m'], kt * 128)
                if DBG1 < 5:
                    return
                if full:
                    qk_simple(hT, 2208, gains['diff_q_g'], roped, scr['QT_d'], qt * 128)
                qk_simple(hT, 2720, gains['diff_k_g'], roped, scr['KT_d'], kt * 128)
                ps = proj(hT, 3232, 3744)
                vsd = vstd_r.next()
                pd3 = v3(ps[:, :], 4)
                O.acopy(vsd[:, :, 0:64], pd3[:, :, 0:64], [ps.res, vsd.res], [vsd.res])
                O.acopy(vsd[:, :, 66:130], pd3[:, :, 64:128], [ps.res, vsd.res], [vsd.res])
                P.dma('sp', scr['V_d'].rearrange("h p t d -> p h t d")[:, :, kt, :], vsd[:], [vsd.res], [], key=vsd.res)

            for i in range(min(2, DBG1_TILES)):
                do_tile(ctxin[i * 128:(i + 1) * 128, :], 1, None, not last, i, i)
            for i in range(min(NLT, max(0, DBG1_TILES - 2))):
                do_tile(xown[i * 128:(i + 1) * 128, :], 0, C['rope_own'][i * 128:(i + 1) * 128, :], True, 2 + i, 2 + i)
            for i in range(min(NLT, max(0, DBG1_TILES - 34))):
                do_tile(xother[i * 128:(i + 1) * 128, :], 0, C['rope_other'][i * 128:(i + 1) * 128, :], both, 2 + NLT + i, 2 + NLT + i)
        P.barrier()
        if DEBUG_STOP == 1:
            return

        with ExitStack() as ph:
            KT_r = sring(ph, 'KT', 2, [128, NK], BF16)
            QT_r = sring(ph, 'QT', 2, [128, NQL], BF16)
            V_r = sring(ph, 'V', 2, [128, NKT * 132], BF16)
            PT_r = sring(ph, 'PT', 4, [128, 1024], BF16)
            oT_r = sring(ph, 'oT', 2, [128, 512], F32)
            om_r = sring(ph, 'om', 2, [128, 4, 128], F32)
            ost_r = sring(ph, 'ost', 3, [128, 4, 128], BF16)
            rc_r = sring(ph, 'rc', 4, [128, 8], F32)
            dt_r = sring(ph, 'dt', 2, [128, 128], F32)
            dsq = sbt(ph, 'dsq', [128, 128], F32)
            NCH = 4
            TPC = (NTBL + NCH - 1) // NCH
            nabf_r = sring(ph, 'nab_f', 2, [128, TPC, 128], F32)
            nam_b = sbt(ph, 'nam_b', [128, NTBL, 128], BF16)
            nab_b = sbt(ph, 'nab_b', [128, NTBL, 128], BF16)
            P.dma('pool', nam_b[:], C['namask'], [], [nam_b.res], key='wl4')

            ps_s = pring(ph, 'ps_s', 3, [128, 512], F32)
            accT_r = pring(ph, 'accT', 2, [128, 512], F32)
            ps_o = pring(ph, 'ptr', 2, [128, 512], F32)

            def vview(V, dvp):
                return V[:, 0:NKT * (dvp + 1)].rearrange("p (t d) -> p t d", d=dvp + 1)

            def load_head(KT_src, QT_src, V_src, nrow, dvp):
                KT = KT_r.next(); QT = QT_r.next(); V = V_r.next()
                P.dma('sp', KT[0:nrow, :], KT_src, [], [KT.res], key=KT.res)
                P.dma('act', QT[0:nrow, 0:NQL], QT_src[:, 0:NQL], [], [QT.res], key=QT.res)
                P.dma('sp', vview(V, dvp), V_src, [], [V.res], key=V.res)
                return KT, QT, V

            def attend_T(KT, QT, V, p0, nrow, q0, nq, kts, scale, dvp, groups):
                Vv = vview(V, dvp)
                nqs = nq // 128
                nk = len(kts)
                accs = [accT_r.next() for _ in groups]
                pend = []

                def pvT(item):
                    idx, kt, pt = item
                    for gi, c0 in enumerate(groups):
                        O.mm(accs[gi][0:65, 0:nq], Vv[:, kt, c0:c0 + 65], pt[:, 0:nq], idx == 0, idx == nk - 1,
                             [pt.res, V.res], [accs[gi].res])
                for idx, kt in enumerate(kts):
                    ps = ps_s.next()
                    O.mm(ps[:, 0:nq], KT[p0:p0 + nrow, kt * 128:(kt + 1) * 128], QT[p0:p0 + nrow, q0:q0 + nq], True, True,
                         [KT.res, QT.res], [ps.res])
                    pt = PT_r.next()
                    O.act(pt[:, 0:nq], ps[:, 0:nq], AF.Exp, [ps.res], [pt.res], scale=scale)
                    pend.append((idx, kt, pt))
                    if len(pend) > 2:
                        pvT(pend.pop(0))
                while pend:
                    pvT(pend.pop(0))
                outs = []
                for gi in range(len(groups)):
                    oT = oT_r.next()
                    O.copy('dve', oT[0:65, 0:nq], accs[gi][0:65, 0:nq], [accs[gi].res], [oT.res])
                    ptr = ps_o.next()
                    for qs in range(nqs):
                        O.tr(ptr[:, qs * 128:qs * 128 + 65], oT[0:65, qs * 128:(qs + 1) * 128], ident_f[0:65, 0:65],
                             [oT.res, ident_f.res], [ptr.res])
                    outs.append(ptr)
                return outs

            def finalize_simple(accs, nqs, dv, ocol, q0):
                ost = ost_r.next()
                rc = rc_r.next()
                for qs in range(nqs):
                    acc, off = accs[qs]
                    O.recip(rc[:, qs:qs + 1], acc[:, off + dv:off + dv + 1], [acc.res, rc.res], [rc.res])
                    O.ts('dve', ost[:, qs, 0:dv], acc[:, off:off + dv], rc[:, qs:qs + 1], ALU.mult, [acc.res, rc.res, ost.res], [ost.res])
                P.dma('sp', scr['O'][q0:q0 + nqs * 128, :].rearrange("(s p) n -> p s n", p=128)[:, :, ocol:ocol + dv],
                      ost[:, 0:nqs, 0:dv], [ost.res], [], key=ost.res)

            all_kts = list(range(NKT))
            for pair in range(4 if DBG2 >= 1 else 0):
                KT = KT_r.next(); QT = QT_r.next()
                P.dma('sp', KT[:, :], scr['KT_na'][pair * 128:(pair + 1) * 128, :], [], [KT.res], key=KT.res)
                P.dma('act', QT[:, 0:NQL], scr['QT_na'][pair * 128:(pair + 1) * 128, 0:NQL], [], [QT.res], key=QT.res)
                for hh in range(2):
                    h = pair * 2 + hh
                    p0 = hh * 64
                    V = V_r.next()
                    Vv = vview(V, 65)
                    P.dma('sp', Vv, scr['V_na'][h], [], [V.res], key=V.res)
                    for c_ in range(NCH):
                        t0_, t1_ = c_ * TPC, min(NTBL, (c_ + 1) * TPC)
                        nf = nabf_r.next()
                        P.dma('sp', nf[:, 0:t1_ - t0_, :], W['nab'][h][:, t0_:t1_, :], [], [nf.res], key=nf.res)
                        O.stt('dve', nab_b[:, t0_:t1_, :], nf[:, 0:t1_ - t0_, :], 1.0 / NA_SCALE, nam_b[:, t0_:t1_, :], ALU.mult, ALU.add,
                              [nf.res, nam_b.res, nab_b.res], [nab_b.res])
                    if not last:
                        (ptr,) = attend_T(KT, QT, V, p0, 64, 0, 256, [0, 1], NA_SCALE, 65, [0])
                        finalize_simple([(ptr, 0), (ptr, 128)], 2, 64, h * 64, 0)
                    qtiles = [('own', i_) for i_ in range(NLT)] + ([('other', i_) for i_ in range(NLT)] if both else [])
                    for grp in range(len(qtiles) // 4):
                        acc = ps_o.next()
                        accs = [(acc, 128 * j) for j in range(4)]
                        for j in range(4):
                            kind, li = qtiles[grp * 4 + j]
                            b_self, b_cross, tb0 = (2, 2 + NLT, 5) if kind == 'own' else (2 + NLT, 2, 29)
                            if 2 <= li <= NLT - 3:
                                slots = [(b_self + li + d, d + 2) for d in range(-2, 3)]
                            elif li < 2:
                                slots = [(b_self + s, tb0 + li * 6 + s) for s in range(4)] + \
                                        [(b_cross + NLT - 2 + s, tb0 + li * 6 + 4 + s) for s in range(2)]
                            else:
                                e_ = li - (NLT - 2) + 2
                                slots = [(b_self + NLT - 4 + s, tb0 + e_ * 6 + s) for s in range(4)] + \
                                        [(b_cross + s, tb0 + e_ * 6 + 4 + s) for s in range(2)]
                            qti = li if kind == 'own' else NLT + li
                            q0 = CTX + qti * 128
                            kts = [0, 1] + [s_[0] for s_ in slots]
                            nsl = len(kts)
                            pss = [ps_s.next(), ps_s.next()]
                            for si, kt in enumerate(kts):
                                ps = pss[si // 4]
                                c0 = (si % 4) * 128
                                tb = None if si < 2 else slots[si - 2][1]
                                O.mm(ps[:, c0:c0 + 128], KT[p0:p0 + 64, kt * 128:(kt + 1) * 128], QT[p0:p0 + 64, q0:q0 + 128],
                                     True, tb is None, [KT.res, QT.res], [ps.res])
                                if tb is not None:
                                    O.mm(ps[:, c0:c0 + 128], ident_b[:], nab_b[:, tb, :], False, True, [ident_b.res, nab_b.res], [ps.res])
                            pt = PT_r.next()
                            O.act(pt[:, 0:512], pss[0][:, :], AF.Exp, [pss[0].res], [pt.res], scale=NA_SCALE)
                            nb = (nsl - 4) * 128
                            O.act(pt[:, 512:512 + nb], pss[1][:, 0:nb], AF.Exp, [pss[1].res, pt.res], [pt.res], scale=NA_SCALE)
                            for si, kt in enumerate(kts):
                                O.mm(acc[:, 128 * j:128 * j + 65], pt[:, si * 128:(si + 1) * 128], Vv[:, kt, 0:65], si == 0, si == nsl - 1,
                                     [pt.res, V.res], [acc.res])
                        finalize_simple(accs, 4, 64, h * 64, CTX + grp * 512)
            for h in range(8 if DBG2 >= 2 else 0):
                KT, QT, V = load_head(scr['KT_m'][h], scr['QT_m'][h], scr['V_m'][h], 96, 65)
                if not last:
                    (ptr,) = attend_T(KT, QT, V, 0, 96, 0, 256, [0, 1], MLA_SCALE, 65, [0])
                    finalize_simple([(ptr, 0), (ptr, 128)], 2, 64, 512 + h * 64, 0)
                for ch in range((NQL - CTX) // 512):
                    (ptr,) = attend_T(KT, QT, V, 0, 96, CTX + ch * 512, 512, all_kts, MLA_SCALE, 65, [0])
                    finalize_simple([(ptr, 128 * j) for j in range(4)], 4, 64, 512 + h * 64, CTX + ch * 512)

            def finalize_diff(om0, om1, nqs, h, q0):
                ost = ost_r.next()
                sgb = gains['diff_subln_g']
                for qs in range(nqs):
                    dt = dt_r.next()
                    rc = rc_r.next()
                    O.stt('dve', dt[:], om1[:, qs, :], nlam_b[:], om0[:, qs, :], ALU.mult, ALU.add, [om0.res, om1.res, nlam_b.res], [dt.res])
                    O.tt('dve', dsq[:], dt[:], dt[:], ALU.mult, [dt.res], [dsq.res])
                    O.red('dve', rc[:, 3:4], dsq[:], [dsq.res, rc.res], [rc.res])
                    O.act(rc[:, 4:5], rc[:, 3:4], AF.Ln, [rc.res, eps_t.res], [rc.res], scale=1.0 / 128, bias=eps_t[:])
                    O.act(rc[:, 5:6], rc[:, 4:5], AF.Exp, [rc.res], [rc.res], scale=-0.5)
                    O.stt('dve', ost[:, qs, :], dt[:], rc[:, 5:6], sgb[:], ALU.mult, ALU.mult, [dt.res, rc.res, sgb.res, ost.res], [ost.res])
                P.dma('sp', scr['O'][q0:q0 + nqs * 128, :].rearrange("(s p) n -> p s n", p=128)[:, :, 1024 + h * 128:1024 + (h + 1) * 128],
                      ost[:, 0:nqs, :], [ost.res], [], key=ost.res)

            for h in range(4 if DBG2 >= 3 else 0):
                KT = KT_r.next(); V = V_r.next()
                P.dma('sp', KT[:, :], scr['KT_d'][h * 128:(h + 1) * 128, :], [], [KT.res], key=KT.res)
                P.dma('sp', vview(V, 131), scr['V_d'][h], [], [V.res], key=V.res)
                QTm = []
                for m in range(2):
                    QTz = QT_r.next()
                    P.dma('act', QTz[:, 0:NQL], scr['QT_d'][h * 128:(h + 1) * 128, 0:NQL], [], [QTz.res], key=QTz.res)
                    O.memset('pool', QTz[(1 - m) * 64:(1 - m) * 64 + 64, 0:NQL], 0.0, [QTz.res])
                    QTm.append(QTz)
                chunks = []
                if not last:
                    chunks.append((0, 256, [0, 1]))
                for ch in range((NQL - CTX) // 512):
                    chunks.append((CTX + ch * 512, 512, all_kts))
                for (q0, nq, kts) in chunks:
                    nqs = nq // 128
                    oms = []
                    for m in range(2):
                        ptrs = attend_T(KT, QTm[m], V, 0, 128, q0, nq, kts, DIFF_SCALE, 131, [0, 66])
                        om = om_r.next()
                        for gi, ptr in enumerate(ptrs):
                            rc = rc_r.next()
                            for qs in range(nqs):
                                O.recip(rc[:, qs:qs + 1], ptr[:, qs * 128 + 64:qs * 128 + 65], [ptr.res, rc.res], [rc.res])
                                O.ts('dve', om[:, qs, gi * 64:(gi + 1) * 64], ptr[:, qs * 128:qs * 128 + 64], rc[:, qs:qs + 1], ALU.mult,
                                     [ptr.res, rc.res, om.res], [om.res])
                        oms.append(om)
                    finalize_diff(oms[0], oms[1], nqs, h, q0)
        P.barrier()
        if DEBUG_STOP == 2:
            return

        with ExitStack() as ph:
            NZ = 4608
            wz = sbt(ph, 'wz', [128, 8, NZ], BF16)
            wsrc = W['w_in'].rearrange("(k p) n -> p k n", p=128)
            for k in range(8):
                P.dma('pool', wz[:, k, :], wsrc[:, k, 3744:D_IN], [], [wz.res + str(k)], key='wl%d' % k)
            wz_res = [wz.res + str(k) for k in range(8)]
            wbr = sbt(ph, 'wbr', [128, 12, 1024], BF16)
            for b_ in range(3):
                P.dma('pool', wbr[:, 4 * b_:4 * b_ + 4, :], W['w_br'][b_].rearrange("(k p) n -> p k n", p=128), [], [wbr.res + str(b_)],
                      key='wl%d' % b_)
            wout = sbt(ph, 'wout', [128, 8, 1024], BF16)
            P.dma('pool', wout[:], W['w_out'].rearrange("(k p) n -> p k n", p=128), [], [wout.res], key='wl3')

            xt_r = sring(ph, 'xt3', 2, [128, 1024], F32)
            o_r = sring(ph, 'o3', 2, [128, 1536], BF16)
            st_r = sring(ph, 'st3', 2, [128, 4], F32)
            et_r = sring(ph, 'et', 2, [128, 512], F32)
            hb = sbt(ph, 'hb3', [128, 1024], BF16)
            hT = sbt(ph, 'hT3', [128, 8, 128], BF16)
            zs = sbt(ph, 'zs', [128, 1536], F32)
            zg = sbt(ph, 'zg', [128, 1536], BF16)
            sgm = sbt(ph, 'sgm', [128, 3072], BF16)
            og = sbt(ph, 'og', [128, 1536], BF16)
            ogT = sbt(ph, 'ogT', [128, 12, 128], BF16)
            yacc = sbt(ph, 'yacc', [128, 1024], F32)
            ytmp = sbt(ph, 'ytmp', [128, 1024], F32)
            yb = sbt(ph, 'yb', [128, 1024], BF16)
            yT = sbt(ph, 'yT', [128, 8, 128], BF16)
            pT = pst(ph, 'pT3', [128, 8, 128], BF16)
            pT2 = pst(ph, 'pT23', [128, 8, 128], BF16)
            pj_r = pring(ph, 'pj3', 4, [128, 512], F32)

            def out_tile(src_ap, which, qrow0, dst_ap):
                xt = xt_r.next()
                P.dma('sp', xt[:], src_ap, [], [xt.res], key=xt.res)
                ot = o_r.next()
                P.dma('act', ot[:], scr['O'][qrow0:qrow0 + 128, :], [], [ot.res], key=ot.res)
                st = st_r.next()
                O.memset('dve', st[:], 0.0, [st.res])
                O.act(yacc[:], xt[:], AF.Square, [xt.res, st.res], [yacc.res, st.res], accum_out=st[:, 0:1])
                O.act(st[:, 1:2], st[:, 0:1], AF.Ln, [st.res, eps_t.res], [st.res], scale=1.0 / 1024, bias=eps_t[:])
                O.act(st[:, 2:3], st[:, 1:2], AF.Exp, [st.res], [st.res], scale=-0.5)
                O.stt('dve', ytmp[:], xt[:], st[:, 2:3], A_[which][:], ALU.mult, ALU.mult, [xt.res, st.res, A_[which].res], [ytmp.res])
                O.tt('pool', hb[:], ytmp[:], Sh_[which][:], ALU.add, [ytmp.res, Sh_[which].res], [hb.res])
                for k in range(8):
                    O.tr(pT[:, k, :], hb[:, k * 128:(k + 1) * 128], ident_b[:], [hb.res, ident_b.res], [pT.res])
                O.acopy(hT[:], pT[:], [pT.res], [hT.res])
                if DBG3 < 2:
                    return
                for g in range(9):
                    ps = pj_r.next()
                    for k in range(8):
                        O.mm(ps[:, :], hT[:, k, :], wz[:, k, g * 512:(g + 1) * 512], k == 0, k == 7, [hT.res, wz_res[k]], [ps.res])
                    et = et_r.next()
                    O.act(et[:], ps[:, :], AF.Exp, [ps.res], [et.res], scale=-1.0)
                    O.ts('dve', et[:], et[:], 1.0, ALU.add, [et.res], [et.res])
                    if g < 3:
                        O.recip(zg[:, g * 512:(g + 1) * 512], et[:], [et.res, zg.res], [zg.res])
                        O.copy('dve', zs[:, g * 512:(g + 1) * 512], ps[:, :], [ps.res, zs.res], [zs.res])
                    else:
                        O.recip(sgm[:, (g - 3) * 512:(g - 2) * 512], et[:], [et.res, sgm.res], [sgm.res])
                if DBG3 < 3:
                    return
                O.tt('pool', zs[:], zs[:], zg[:], ALU.mult, [zs.res, zg.res], [zs.res])
                O.tt('dve', og[:], zs[:], ot[:], ALU.mult, [zs.res, ot.res], [og.res])
                if DBG3 < 4:
                    return
                for c in range(12):
                    pp = pT if c < 8 else pT2
                    O.tr(pp[:, c % 8, :], og[:, c * 128:(c + 1) * 128], ident_b[:], [og.res, ident_b.res], [pp.res])
                O.acopy(ogT[:, 0:8, :], pT[:], [pT.res, ogT.res], [ogT.res])
                O.copy('dve', ogT[:, 8:12, :], pT2[:, 0:4, :], [pT2.res, ogT.res], [ogT.res])
                if DBG3 < 5:
                    return
                for b_ in range(3):
                    for g in range(2):
                        ps = pj_r.next()
                        for k in range(4):
                            O.mm(ps[:, :], ogT[:, 4 * b_ + k, :], wbr[:, 4 * b_ + k, g * 512:(g + 1) * 512], k == 0, k == 3,
                                 [ogT.res, wbr.res + str(b_)], [ps.res])
                        gsl = sgm[:, b_ * 1024 + g * 512:b_ * 1024 + (g + 1) * 512]
                        ysl = yacc[:, g * 512:(g + 1) * 512]
                        tsl = ytmp[:, g * 512:(g + 1) * 512]
                        if b_ == 0:
                            O.tt('dve', ysl, ps[:, :], gsl, ALU.mult, [ps.res, sgm.res, yacc.res], [yacc.res])
                        else:
                            O.tt('dve', tsl, ps[:, :], gsl, ALU.mult, [ps.res, sgm.res, ytmp.res], [ytmp.res])
                            if b_ == 1:
                                O.tt('pool', ysl, ysl, tsl, ALU.add, [ytmp.res, yacc.res], [yacc.res])
                            else:
                                O.tt('pool', yb[:, g * 512:(g + 1) * 512], ysl, tsl, ALU.add, [ytmp.res, yacc.res, yb.res], [yb.res])
                if DBG3 < 6:
                    return
                for k in range(8):
                    O.tr(pT[:, k, :], yb[:, k * 128:(k + 1) * 128], ident_b[:], [yb.res, ident_b.res], [pT.res])
                O.acopy(yT[:], pT[:], [pT.res], [yT.res])
                for g in range(2):
                    ps = pj_r.next()
                    for k in range(8):
                        O.mm(ps[:, :], yT[:, k, :], wout[:, k, g * 512:(g + 1) * 512], k == 0, k == 7, [yT.res, wout.res], [ps.res])
                    O.tt('dve', ytmp[:, g * 512:(g + 1) * 512], ps[:, :], Gt_[which][:, g * 512:(g + 1) * 512], ALU.mult,
                         [ps.res, Gt_[which].res, ytmp.res], [ytmp.res])
                if DBG3 < 7:
                    return
                O.tt('pool', xt[:], xt[:], ytmp[:], ALU.add, [xt.res, ytmp.res], [xt.res])
                P.dma('sp', dst_ap, xt[:], [xt.res], [], key=xt.res + 'o')

            if not last:
                for i in range(min(2, DBG3_TILES)):
                    out_tile(ctxin[i * 128:(i + 1) * 128, :], 1, i * 128, ctxout[i * 128:(i + 1) * 128, :])
            for i in range(min(NLT, max(0, DBG3_TILES - 2))):
                out_tile(xown[i * 128:(i + 1) * 128, :], 0, CTX + i * 128, xout[i * 128:(i + 1) * 128, :])
            if both:
                for i in range(NLT):
                    out_tile(xother[i * 128:(i + 1) * 128, :], 0, CTX + (NLT + i) * 128, xout_other[i * 128:(i + 1) * 128, :])
        P.barrier()


def declare_scratch(nc, sfx):
    s = {}
    def d(name, shape):
        s[name] = nc.dram_tensor('scr_' + name + sfx, shape, BF16, kind=("ExternalOutput" if DEBUG_SCR else "Internal")).ap()
    d('QT_na', [512, NQ2]); d('KT_na', [512, NK]); d('V_na', [8, 128, NKT, 66])
    d('QT_m', [8, 96, NQ2]); d('KT_m', [8, 96, NK]); d('V_m', [8, 128, NKT, 66])
    d('QT_d', [512, NQ2]); d('KT_d', [512, NK]); d('V_d', [4, 128, NKT, 132])
    d('O', [NQ2, 1536])
    return s


def build_fused():
    nc = bass.Bass("TRN2", target_bir_lowering=False)
    C = {}
    for name, shape in (('cvec', [1024]), ('cctx', [1024]), ('rope_own', [HALF, 96]), ('rope_other', [HALF, 96]),
                        ('namask', [128, NTBL, 128]), ('ident', [128, 128])):
        C[name] = nc.dram_tensor(name, shape, F32, kind="ExternalInput").ap()
    xown = nc.dram_tensor('xown', [HALF, 1024], F32, kind="ExternalInput").ap()
    xother = nc.dram_tensor('xother', [HALF, 1024], F32, kind="ExternalInput").ap()
    ctxin = nc.dram_tensor('ctxin', [CTX, 1024], F32, kind="ExternalInput").ap()
    xout = nc.dram_tensor('xout', [HALF, 1024], F32, kind="ExternalOutput").ap()
    x1own = nc.dram_tensor('x1own', [HALF, 1024], F32, kind="Internal").ap()
    x1other = nc.dram_tensor('x1other', [HALF, 1024], F32, kind="Internal").ap()
    ctx1 = nc.dram_tensor('ctx1', [CTX, 1024], F32, kind="Internal").ap()
    W0 = declare_layer_weights(nc, '_0')
    W1 = declare_layer_weights(nc, '_1')
    scr = declare_scratch(nc, '')
    P = Prog(nc)
    with ExitStack() as st:
        emit_layer(nc, P, 0, False, W0, C, xown, xother, ctxin, x1own, ctx1, scr, both=True, xout_other=x1other)
        emit_layer(nc, P, 1, True, W1, C, x1own, x1other, ctx1, xout, None, scr)
        P.emit(st)
    return nc, P


def rope_table():
    t = np.arange(SEQ)
    row = (t // GRID_W).astype(np.float32)
    col = (t % GRID_W).astype(np.float32)
    out = np.zeros((SEQ, 96), np.float32)
    for rot, off in ((32, 0), (64, 32)):
        nf = rot // 4
        inv = (10000.0 ** (-np.arange(nf, dtype=np.float32) / nf)).astype(np.float32)
        ang = np.concatenate([row[:, None] * inv, col[:, None] * inv], axis=-1).astype(np.float32)
        hh = rot // 2
        out[:, off:off + hh] = np.cos(ang)
        out[:, off + hh:off + 2 * hh] = np.sin(ang)
    return out


def na_tables(half):
    def table(i, j):
        kr = np.arange(128) // 64; kc = np.arange(128) % 64
        qr = np.arange(128) // 64; qc = np.arange(128) % 64
        r = (2 * i + qr)[None, :]
        krow = (2 * j + kr)[:, None]
        rs = np.clip(r - 4, 0, 120)
        vrow = (krow >= rs) & (krow <= rs + 7)
        cstart = np.clip(qc - 8, 0, 48)[None, :]
        vcol = (kc[:, None] >= cstart) & (kc[:, None] < cstart + 16)
        dr = np.clip(krow - r + 7, 0, 14)
        dc = np.clip(kc[:, None] - qc[None, :], -15, 15) + 15
        valid = vrow & vcol & (j >= 0) & (j < 64)
        return np.broadcast_to(dr, (128, 128)), dc, valid
    tabs = []
    for d in range(-2, 3):
        tabs.append(table(10, 10 + d))
    own_g = lambda li: half * NLT + li
    oth_g = lambda oi: (1 - half) * NLT + oi
    for self_g, cross_g in ((own_g, oth_g), (oth_g, own_g)):
        for li in (0, 1, NLT - 2, NLT - 1):
            i = self_g(li)
            if li < 2:
                keys = [self_g(s_) for s_ in range(4)] + [cross_g(NLT - 2 + s_) for s_ in range(2)]
            else:
                keys = [self_g(NLT - 4 + s_) for s_ in range(4)] + [cross_g(s_) for s_ in range(2)]
            for j in keys:
                tabs.append(table(i, j))
    dr = np.stack([t[0] for t in tabs]); dc = np.stack([t[1] for t in tabs]); va = np.stack([t[2] for t in tabs])
    return dr, dc, va


_CACHE = {}


def _get_prog():
    if 'f' not in _CACHE:
        _CACHE['f'] = build_fused()
    return _CACHE['f'][0]


def kernel(x, c, ctx, c_ctx, norm_g, w_ada, b_ada, w_in, na_rpb, na_q_g, na_k_g, mla_cq_g, mla_ckv_g, w_uq, w_ukv,
           mla_q_g, mla_k_g, diff_q_g, diff_k_g, diff_lq1, diff_lk1, diff_lq2, diff_lk2, diff_subln_g, w_br, w_out):
    f = lambda a: np.ascontiguousarray(np.asarray(a, dtype=np.float32))
    x = f(x); c = f(c); ctx = f(ctx); c_ctx = f(c_ctx)
    na_rpb = f(na_rpb)
    rt = rope_table()
    ident = np.eye(128, dtype=np.float32)
    wmaps = {}
    for l in range(DEPTH):
        p = {
            'norm_g': f(norm_g[l]), 'w_ada': f(w_ada[l]), 'b_ada': f(b_ada[l]), 'w_in': f(w_in[l]),
            'na_q_g': f(na_q_g[l]), 'na_k_g': f(na_k_g[l]), 'mla_cq_g': f(mla_cq_g[l]), 'mla_ckv_g': f(mla_ckv_g[l]),
            'w_uq': f(w_uq[l]), 'w_ukv': f(w_ukv[l]), 'mla_q_g': f(mla_q_g[l]), 'mla_k_g': f(mla_k_g[l]),
            'diff_q_g': f(diff_q_g[l]), 'diff_k_g': f(diff_k_g[l]),
            'diff_l': np.ascontiguousarray(np.stack([f(diff_lq1[l]), f(diff_lk1[l]), f(diff_lq2[l]), f(diff_lk2[l])])),
            'diff_subln_g': f(diff_subln_g[l]), 'w_br': f(w_br[l]), 'w_out': f(w_out[l]),
        }
        for k_, v in p.items():
            wmaps['%s_%d' % (k_, l)] = v
    per_half = []
    for half in range(2):
        dr, dc, va = na_tables(half)
        mask = np.where(va, 0.0, MASKVAL).astype(np.float32)
        d = {'rope_own': np.ascontiguousarray(rt[half * HALF:(half + 1) * HALF]),
             'rope_other': np.ascontiguousarray(rt[(1 - half) * HALF:(2 - half) * HALF]),
             'namask': np.ascontiguousarray(mask.transpose(1, 0, 2))}
        for l in range(DEPTH):
            g = na_rpb[l][:, dr, dc]
            g = np.where(va[None], g, np.float32(0.0))
            d['nab_%d' % l] = np.ascontiguousarray(g.transpose(0, 2, 1, 3))
        per_half.append(d)
    maps = []
    for core in range(8):
        b, half = core // 2, core % 2
        m = {'xown': np.ascontiguousarray(x[b, half * HALF:(half + 1) * HALF]),
             'xother': np.ascontiguousarray(x[b, (1 - half) * HALF:(2 - half) * HALF]),
             'ctxin': np.ascontiguousarray(ctx[b]), 'cvec': np.ascontiguousarray(c[b]), 'cctx': c_ctx, 'ident': ident}
        m.update(per_half[half])
        m.update(wmaps)
        maps.append(m)
    nc = _get_prog()
    res = run_bass_kernel_spmd(nc, maps, core_ids=list(range(8)))
    out = np.empty_like(x)
    for core in range(8):
        b, half = core // 2, core % 2
        out[b, half * HALF:(half + 1) * HALF] = res.results[core]['xout']
    return out
```

```python
import math
from contextlib import ExitStack

import numpy as np
import concourse.bass as bass
import concourse.mybir as mybir
from concourse.bass_utils import run_bass_kernel_spmd

F32 = mybir.dt.float32
BF16 = mybir.dt.bfloat16
AF = mybir.ActivationFunctionType
ALU = mybir.AluOpType
AX = mybir.AxisListType

D_MODEL = 1024
BATCH = 4
SEQ = 8192
DEPTH = 2
GRID_W = 64
CTX = 256
EPS = 1e-6
D_IN = 8352
NA_SCALE = 64 ** -0.5
MLA_SCALE = 96 ** -0.5
DIFF_SCALE = 64 ** -0.5
HALF = SEQ // 2
NQ = CTX + HALF
NQ2 = CTX + SEQ
NK = CTX + SEQ
NKT = NK // 128
NLT = HALF // 128
NTBL = 5 + 24 + 24
MASKVAL = -240000.0

ENGS = ['pe', 'act', 'dve', 'pool', 'sp']
DEBUG_STOP = None
DEBUG_SCR = False
DBG0 = 99
DBG1 = 99
DBG1_TILES = 99
DBG2 = 99
DBG3 = 99
DBG3_TILES = 99
DBG3X = 99


class Op:
    __slots__ = ('eng', 'fn', 'deps', 'is_dma', 'key', 'signal', 'sig_idx', 'dma_val')

    def __init__(self, eng, fn, is_dma, key):
        self.eng = eng
        self.fn = fn
        self.deps = []
        self.is_dma = is_dma
        self.key = key
        self.signal = is_dma
        self.sig_idx = None
        self.dma_val = None


class Prog:
    def __init__(self, nc):
        self.nc = nc
        self.q = {e: [] for e in ENGS}
        self.last_w = {}
        self.readers = {}
        self.dma_count = {}
        self.dma_last = {}
        self.nops = 0

    def _dep(self, op, prod):
        if prod is None or prod is op:
            return
        if (not prod.is_dma) and (not op.is_dma) and prod.eng == op.eng and op.eng in ('pe', 'sp'):
            return
        if prod not in op.deps:
            op.deps.append(prod)
            prod.signal = True

    def add(self, eng, fn, reads=(), writes=(), dma=False, key=None):
        op = Op(eng, fn, dma, key)
        for r in reads:
            self._dep(op, self.last_w.get(r))
        for w in writes:
            self._dep(op, self.last_w.get(w))
            for rd in self.readers.get(w, ()):
                self._dep(op, rd)
        for r in reads:
            self.readers.setdefault(r, []).append(op)
        for w in writes:
            self.last_w[w] = op
            self.readers[w] = []
        if dma:
            prev = self.dma_last.get(key)
            if prev is not None:
                self._dep(op, prev)
            self.dma_last[key] = op
            n = self.dma_count.get(key, 0) + 1
            self.dma_count[key] = n
            op.dma_val = 16 * n
        self.q[eng].append(op)
        self.nops += 1
        return op

    def dma(self, eng, out, in_, reads, writes, key):
        return self.add(eng, lambda e: e.dma_start(out=out, in_=in_), reads, writes, dma=True, key=key)

    def barrier(self):
        lasts = [self.q[e][-1] for e in ENGS if self.q[e]]
        lasts = [p for p in lasts if p.fn is not None]
        dmas = list(self.dma_last.values())
        for e in ENGS:
            op = Op(e, None, False, None)
            for p in lasts + dmas:
                if p.is_dma or p.eng != e:
                    if p not in op.deps:
                        op.deps.append(p)
                        p.signal = True
            self.q[e].append(op)
        self.last_w = {}
        self.readers = {}

    def finish(self, res):
        op = Op('sp', None, False, None)
        for r in res:
            p = self.last_w.get(r)
            if p is not None and p not in op.deps:
                op.deps.append(p)
                p.signal = True
        self.q['sp'].append(op)

    def emit(self, stack):
        nc = self.nc
        for e in ENGS:
            c = 0
            for op in self.q[e]:
                if not op.is_dma and op.signal:
                    c += 1
                    op.sig_idx = c
        esem = {e: stack.enter_context(nc.semaphore('s_' + e)) for e in ENGS if e != 'sp'}
        dsem = {k: stack.enter_context(nc.semaphore('d%d' % i)) for i, k in enumerate(self.dma_count)}
        self.n_sems = len(esem) + len(dsem)
        block = stack.enter_context(nc.Block())
        q = self.q

        def run(e, handle):
            seen = {}
            for op in q[e]:
                for p in op.deps:
                    if p.is_dma:
                        s, v = dsem[p.key], p.dma_val
                    else:
                        s, v = esem[p.eng], p.sig_idx
                    sid = id(s)
                    if seen.get(sid, 0) >= v:
                        continue
                    seen[sid] = v
                    handle.wait_ge(s, v)
                if op.fn is None:
                    continue
                ins = op.fn(handle)
                if op.is_dma:
                    ins.then_inc(dsem[op.key], 16)
                elif op.signal:
                    ins.then_inc(esem[e], 1)

        @block.tensor
        def _(h):
            run('pe', h)

        @block.scalar
        def _(h):
            run('act', h)

        @block.vector
        def _(h):
            run('dve', h)

        @block.gpsimd
        def _(h):
            run('pool', h)

        @block.sync
        def _(h):
            run('sp', h)


class T:
    def __init__(self, ap, res):
        self.t = ap
        self.res = res

    def __getitem__(self, k):
        return self.t[k]


class Ring:
    def __init__(self, tiles):
        self.tiles = tiles
        self.i = 0

    def next(self):
        t = self.tiles[self.i % len(self.tiles)]
        self.i += 1
        return t


def declare_layer_weights(nc, sfx):
    w = {}

    def d(name, shape):
        w[name] = nc.dram_tensor(name + sfx, shape, F32, kind="ExternalInput").ap()
    d('norm_g', [1024]); d('w_ada', [1024, 3072]); d('b_ada', [3072]); d('w_in', [1024, D_IN])
    d('nab', [8, 128, NTBL, 128])
    d('na_q_g', [64]); d('na_k_g', [64]); d('mla_cq_g', [384]); d('mla_ckv_g', [256])
    d('w_uq', [384, 768]); d('w_ukv', [256, 1024]); d('mla_q_g', [96]); d('mla_k_g', [96])
    d('diff_q_g', [64]); d('diff_k_g', [64]); d('diff_l', [4, 64]); d('diff_subln_g', [128])
    d('w_br', [3, 512, 1024]); d('w_out', [1024, 1024])
    return w


class Ops:
    def __init__(self, P):
        self.P = P

    def mm(self, out, lhsT, rhs, start, stop, reads, writes, skip=False):
        self.P.add('pe', lambda e: e.matmul(out, lhsT=lhsT, rhs=rhs, start=start, stop=stop, skip_group_check=skip), reads, writes)

    def tr(self, out, in_, ident, reads, writes):
        self.P.add('pe', lambda e: e.transpose(out=out, in_=in_, identity=ident), reads, writes)

    def act(self, out, in_, func, reads, writes, **kw):
        self.P.add('act', lambda e: e.activation(out=out, in_=in_, func=func, **kw), reads, writes)

    def acopy(self, out, in_, reads, writes):
        self.P.add('act', lambda e: e.copy(out=out, in_=in_), reads, writes)

    def copy(self, eng, out, in_, reads, writes):
        self.P.add(eng, lambda e: e.tensor_copy(out=out, in_=in_), reads, writes)

    def tt(self, eng, out, in0, in1, op, reads, writes):
        self.P.add(eng, lambda e: e.tensor_tensor(out=out, in0=in0, in1=in1, op=op), reads, writes)

    def ts(self, eng, out, in0, s1, op0, reads, writes, s2=None, op1=None):
        if op1 is None:
            self.P.add(eng, lambda e: e.tensor_scalar(out=out, in0=in0, scalar1=s1, scalar2=None, op0=op0), reads, writes)
        else:
            self.P.add(eng, lambda e: e.tensor_scalar(out=out, in0=in0, scalar1=s1, scalar2=s2, op0=op0, op1=op1), reads, writes)

    def stt(self, eng, out, in0, scalar, in1, op0, op1, reads, writes):
        self.P.add(eng, lambda e: e.scalar_tensor_tensor(out=out, in0=in0, scalar=scalar, in1=in1, op0=op0, op1=op1), reads, writes)

    def red(self, eng, out, in_, reads, writes):
        self.P.add(eng, lambda e: e.tensor_reduce(out=out, in_=in_, axis=AX.X, op=ALU.add), reads, writes)

    def recip(self, out, in_, reads, writes):
        nc = self.P.nc

        def fn(e):
            with nc.allow_low_precision(reason="fp32 reciprocal rounded once to the bf16 consumer dtype"):
                return e.reciprocal(out=out, in_=in_)
        self.P.add('dve', fn, reads, writes)

    def memset(self, eng, ap, val, writes):
        self.P.add(eng, lambda e: e.memset(ap, val), [], writes)


def emit_layer(nc, P, l, last, W, C, xown, xother, ctxin, xout, ctxout, scr, both=False, xout_other=None):
    lam_init = 0.8 - 0.6 * math.exp(-0.3 * l)
    L = 'L%d_' % l
    NQL = NQ2 if both else NQ
    O = Ops(P)

    def sbt(stack, name, shape, dt):
        return T(stack.enter_context(nc.sbuf_tensor(L + name, shape, dt)), L + name)

    def pst(stack, name, shape, dt):
        return T(stack.enter_context(nc.psum_tensor(L + name, shape, dt)), L + name)

    def sring(stack, name, n, shape, dt):
        return Ring([sbt(stack, '%s%d' % (name, i), shape, dt) for i in range(n)])

    def pring(stack, name, n, shape, dt):
        return Ring([pst(stack, '%s%d' % (name, i), shape, dt) for i in range(n)])

    def v3(ap, h):
        return ap.rearrange("p (h d) -> p h d", h=h)

    with ExitStack() as LS:
        ident_f = sbt(LS, 'ident_f', [128, 128], F32)
        ident_b = sbt(LS, 'ident_b', [128, 128], BF16)
        ones_f = sbt(LS, 'ones_f', [1, 128], F32)
        eps_t = sbt(LS, 'eps_t', [128, 1], F32)
        A_ = [sbt(LS, 'A%d' % i, [128, 1024], F32) for i in range(2)]
        Sh_ = [sbt(LS, 'Sh%d' % i, [128, 1024], F32) for i in range(2)]
        Gt_ = [sbt(LS, 'Gt%d' % i, [128, 1024], F32) for i in range(2)]
        nlam_b = sbt(LS, 'nlam_b', [128, 1], F32)
        gains = {}
        for nm, n in (('na_q_g', 64), ('na_k_g', 64), ('mla_cq_g', 384), ('mla_ckv_g', 256), ('mla_q_g', 96),
                      ('mla_k_g', 96), ('diff_q_g', 64), ('diff_k_g', 64), ('diff_subln_g', 128)):
            gains[nm] = sbt(LS, nm, [128, n], F32)
            P.dma('sp', gains[nm][:], W[nm].partition_broadcast(128), [], [gains[nm].res], key='small')
        P.dma('sp', ident_f[:], C['ident'], [], [ident_f.res], key='small')
        O.copy('dve', ident_b[:], ident_f[:], [ident_f.res], [ident_b.res])
        O.memset('dve', ones_f[:], 1.0, [ones_f.res])
        O.memset('dve', eps_t[:], EPS, [eps_t.res])
        sg = gains['diff_subln_g']
        O.ts('dve', sg[:], sg[:], (1.0 - lam_init), ALU.mult, [sg.res], [sg.res])

        with ExitStack() as ph:
            wada = sbt(ph, 'wada', [128, 8, 3072], F32)
            wsrc = W['w_ada'].rearrange("(p k) n -> p k n", k=8)
            for i in range(4):
                P.dma('sp' if i % 2 == 0 else 'act', wada[:, 2 * i:2 * i + 2, :], wsrc[:, 2 * i:2 * i + 2, :], [],
                      [wada.res + str(i)], key='wada%d' % i)
            ccol = sbt(ph, 'ccol', [128, 2, 8], F32)
            P.dma('sp', ccol[:, 0, :], C['cvec'].rearrange("(p k) -> p k", k=8), [], [ccol.res + 'a'], key='small')
            P.dma('sp', ccol[:, 1, :], C['cctx'].rearrange("(p k) -> p k", k=8), [], [ccol.res + 'b'], key='small')
            sig = sbt(ph, 'sig', [128, 2, 8], F32)
            scol = sbt(ph, 'scol', [128, 2, 8], F32)
            O.act(sig[:], ccol[:], AF.Sigmoid, [ccol.res + 'a', ccol.res + 'b'], [sig.res])
            O.tt('dve', scol[:], ccol[:], sig[:], ALU.mult, [sig.res, ccol.res + 'a', ccol.res + 'b'], [scol.res])
            brow = sbt(ph, 'brow', [1, 3072], F32)
            P.dma('sp', brow[:], W['b_ada'].rearrange("(o n) -> o n", o=1), [], [brow.res], key='small')
            gnb = sbt(ph, 'gnb', [128, 1024], F32)
            P.dma('sp', gnb[:], W['norm_g'].partition_broadcast(128), [], [gnb.res], key='small')
            modrow = [sbt(ph, 'modrow%d' % i, [1, 3072], F32) for i in range(2)]
            pmod = pring(ph, 'pmod', 2, [128, 512], F32)
            for which in range(2 if DBG0 >= 2 else 0):
                for cg in range(6):
                    ps = pmod.next()
                    for k in range(8):
                        O.mm(ps[0:1, :], scol[:, which, k:k + 1], wada[:, k, cg * 512:(cg + 1) * 512], k == 0, k == 7,
                             [scol.res, wada.res + str(k // 2)], [ps.res])
                    O.tt('dve', modrow[which][0:1, cg * 512:(cg + 1) * 512], ps[0:1, :], brow[0:1, cg * 512:(cg + 1) * 512],
                         ALU.add, [ps.res, brow.res], [modrow[which].res])
            for which in range(2 if DBG0 >= 3 else 0):
                for cg in range(6):
                    ps = pmod.next()
                    O.mm(ps[:, :], ones_f[0:1, :], modrow[which][0:1, cg * 512:(cg + 1) * 512], True, True,
                         [ones_f.res, modrow[which].res], [ps.res])
                    c0 = (cg % 2) * 512
                    if cg < 2:
                        O.acopy(Sh_[which][:, c0:c0 + 512], ps[:, :], [ps.res], [Sh_[which].res])
                    elif cg < 4:
                        O.stt('dve', A_[which][:, c0:c0 + 512], ps[:, :], 1.0, gnb[:, c0:c0 + 512], ALU.add, ALU.mult,
                              [ps.res, gnb.res], [A_[which].res])
                    else:
                        O.acopy(Gt_[which][:, c0:c0 + 512], ps[:, :], [ps.res], [Gt_[which].res])
            if DBG0 < 4:
                P.barrier()
                return
            lrow = sbt(ph, 'lrow', [128, 4, 64], F32)
            P.dma('sp', lrow[:], W['diff_l'].rearrange("a d -> (a d)").partition_broadcast(128), [], [lrow.res], key='small')
            lprod = sbt(ph, 'lprod', [128, 2, 64], F32)
            lsum = sbt(ph, 'lsum', [128, 8], F32)
            O.tt('dve', lprod[:, 0, :], lrow[:, 0, :], lrow[:, 1, :], ALU.mult, [lrow.res], [lprod.res])
            O.tt('dve', lprod[:, 1, :], lrow[:, 2, :], lrow[:, 3, :], ALU.mult, [lrow.res, lprod.res], [lprod.res])
            O.red('dve', lsum[:, 0:2], lprod[:], [lprod.res], [lsum.res])
            O.act(lsum[:, 2:4], lsum[:, 0:2], AF.Exp, [lsum.res], [lsum.res])
            O.tt('dve', lsum[:, 4:5], lsum[:, 3:4], lsum[:, 2:3], ALU.subtract, [lsum.res], [lsum.res])
            O.ts('dve', nlam_b[:], lsum[:, 4:5], -lam_init, ALU.add, [lsum.res], [nlam_b.res])
        P.barrier()
        if DEBUG_STOP == 0 and DBG0 == 4:
            return
        if DEBUG_STOP == 0:
            for i, tl in enumerate([A_[0], Sh_[0], Gt_[0], A_[1], Sh_[1], Gt_[1]][:DBG0 - 4]):
                P.dma('sp', xout[i * 128:(i + 1) * 128, :], tl[:], [], [], key='dbg')
            P.barrier()
            return

        with ExitStack() as ph:
            NQKV = 3744
            wq = sbt(ph, 'wq', [128, 8, NQKV], BF16)
            wsrc = W['w_in'].rearrange("(k p) n -> p k n", p=128)
            for k in range(8):
                P.dma('pool', wq[:, k, :], wsrc[:, k, 0:NQKV], [], [wq.res + str(k)], key='wl%d' % k)
            wq_res = [wq.res + str(k) for k in range(8)]
            wuq = sbt(ph, 'wuq', [128, 3, 768], BF16)
            P.dma('pool', wuq[:], W['w_uq'].rearrange("(k p) n -> p k n", p=128), [], [wuq.res], key='wl0')
            wukv = sbt(ph, 'wukv', [128, 2, 1024], BF16)
            P.dma('pool', wukv[:], W['w_ukv'].rearrange("(k p) n -> p k n", p=128), [], [wukv.res], key='wl1')

            xt_r = sring(ph, 'xt', 2, [128, 1024], F32)
            rope_r = sring(ph, 'rope', 2, [128, 96], F32)
            junk = sbt(ph, 'junk', [128, 1024], F32)
            st_r = sring(ph, 'st', 2, [128, 4], F32)
            h1 = sbt(ph, 'h1', [128, 1024], F32)
            hb = sbt(ph, 'hb', [128, 1024], BF16)
            hT_r = sring(ph, 'hT', 2, [128, 8, 128], BF16)
            sq_r = sring(ph, 'sq', 2, [128, 768], F32)
            ss_r = sring(ph, 'ss', 3, [128, 24], F32)
            t_r = sring(ph, 't', 2, [128, 768], F32)
            tg_r = sring(ph, 'tg', 3, [128, 768], F32)
            dst_r = sring(ph, 'dst', 8, [128, 768], BF16)
            rtmp = [sbt(ph, 'rtmp%d' % i, [128, 256], F32) for i in range(4)]
            mkraw = sbt(ph, 'mkraw', [128, 8, 96], F32)
            krs_r = sring(ph, 'krs', 3, [128, 32], F32)
            carry = []
            cT_r = sring(ph, 'cT', 3, [128, 3, 128], BF16)
            stg4_r = sring(ph, 'stg4', 3, [128, 4, 128], BF16)
            stg8_r = sring(ph, 'stg8', 2, [96, 8, 128], BF16)
            vst_r = sring(ph, 'vst', 3, [128, 8, 66], BF16)
            vstd_r = sring(ph, 'vstd', 2, [128, 4, 132], BF16)
            for tl in vst_r.tiles + vstd_r.tiles:
                O.memset('pool', tl[:], 1.0, [tl.res])

            pT = pst(ph, 'pT', [128, 8, 128], BF16)
            pj_r = pring(ph, 'pj', 4, [128, 512], F32)
            ptq_r = pring(ph, 'ptq', 2, [128, 8, 128], BF16)

            def headnorm(src_ap, src_res, H, D, gain):
                n = H * D
                sq = sq_r.next(); ss = ss_r.next(); t = t_r.next(); tg = tg_r.next()
                O.act(sq[:, :n], src_ap, AF.Square, [src_res], [sq.res])
                O.red('dve', ss[:, 0:H], v3(sq[:, :n], H), [sq.res], [ss.res])
                O.act(ss[:, 8:8 + H], ss[:, 0:H], AF.Ln, [ss.res, eps_t.res], [ss.res], scale=1.0 / D, bias=eps_t[:])
                O.act(ss[:, 16:16 + H], ss[:, 8:8 + H], AF.Exp, [ss.res], [ss.res], scale=-0.5)
                O.tt('dve', v3(t[:, :n], H), v3(src_ap, H), ss[:, 16:16 + H].unsqueeze(2).to_broadcast([128, H, D]), ALU.mult,
                     [src_res, ss.res], [t.res])
                O.tt('pool', v3(tg[:, :n], H), v3(t[:, :n], H), gain[:, 0:D].unsqueeze(1).to_broadcast([128, H, D]), ALU.mult,
                     [t.res, gain.res], [tg.res])
                return tg

            def rope_into(t3, t_res, d3, d_res, H, r0, R, rope):
                if rope is None:
                    O.acopy(d3, t3, [t_res, d_res], [d_res])
                    return
                rt, co, so = rope
                hh = R // 2
                if r0 > 0:
                    O.acopy(d3[:, :, 0:r0], t3[:, :, 0:r0], [t_res, d_res], [d_res])
                x1 = t3[:, :, r0:r0 + hh]
                x2 = t3[:, :, r0 + hh:r0 + R]
                cs = rt[:, co:co + hh].unsqueeze(1).to_broadcast([128, H, hh])
                sn = rt[:, so:so + hh].unsqueeze(1).to_broadcast([128, H, hh])
                ra, rb, rc, rd = [v3(x[:, :H * hh], H) for x in rtmp]
                O.tt('dve', ra, x1, cs, ALU.mult, [t_res, rt.res], [rtmp[0].res])
                O.tt('pool', rb, x2, sn, ALU.mult, [t_res, rt.res], [rtmp[1].res])
                O.tt('pool', rc, x2, cs, ALU.mult, [t_res, rt.res], [rtmp[2].res])
                O.tt('dve', rd, x1, sn, ALU.mult, [t_res, rt.res], [rtmp[3].res])
                O.tt('dve', d3[:, :, r0:r0 + hh], ra, rb, ALU.subtract, [rtmp[0].res, rtmp[1].res, d_res], [d_res])
                O.tt('pool', d3[:, :, r0 + hh:r0 + R], rc, rd, ALU.add, [rtmp[2].res, rtmp[3].res, d_res], [d_res])

            def tstore4(dst, scr_ap, col0):
                pq = ptq_r.next()
                for c in range(4):
                    O.tr(pq[:, c, :], dst[:, c * 128:(c + 1) * 128], ident_b[:], [dst.res, ident_b.res], [pq.res])
                stg = stg4_r.next()
                O.acopy(stg[:], pq[:, 0:4, :], [pq.res], [stg.res])
                P.dma('sp', scr_ap.rearrange("(c p) n -> p c n", p=128)[:, :, col0:col0 + 128], stg[:], [stg.res], [], key=stg.res)

            def tstore8(dst, scr_ap, col0):
                pq = ptq_r.next()
                for c in range(8):
                    O.tr(pq[0:96, c, :], dst[:, c * 96:(c + 1) * 96], ident_b[:], [dst.res, ident_b.res], [pq.res])
                stg = stg8_r.next()
                O.acopy(stg[:], pq[0:96, :, :], [pq.res], [stg.res])
                P.dma('sp', scr_ap.rearrange("h p n -> p h n")[:, :, col0:col0 + 128], stg[:], [stg.res], [], key=stg.res)

            def proj(hT, c0, c1):
                ps = pj_r.next()
                n = c1 - c0
                for k in range(8):
                    O.mm(ps[:, 0:n], hT[:, k, :], wq[:, k, c0:c1], k == 0, k == 7, [hT.res, wq_res[k]], [ps.res])
                return ps

            def qk_simple(hT, c0, gain, rope, scr_ap, col0):
                ps = proj(hT, c0, c0 + 512)
                tg = headnorm(ps[:, :], ps.res, 8, 64, gain)
                dst = dst_r.next()
                rope_into(v3(tg[:, :512], 8), tg.res, v3(dst[:, :512], 8), dst.res, 8, 0, 64, rope)
                tstore4(dst, scr_ap, col0)

            def do_tile(src_ap, which, rope_src, full, kt, qt):
                xt = xt_r.next()
                P.dma('sp', xt[:], src_ap, [], [xt.res], key=xt.res)
                rp = None
                if rope_src is not None:
                    rp = rope_r.next()
                    P.dma('sp', rp[:], rope_src, [], [rp.res], key=rp.res)
                st = st_r.next()
                O.memset('dve', st[:], 0.0, [st.res])
                O.act(junk[:], xt[:], AF.Square, [xt.res, st.res], [junk.res, st.res], accum_out=st[:, 0:1])
                O.act(st[:, 1:2], st[:, 0:1], AF.Ln, [st.res, eps_t.res], [st.res], scale=1.0 / 1024, bias=eps_t[:])
                O.act(st[:, 2:3], st[:, 1:2], AF.Exp, [st.res], [st.res], scale=-0.5)
                O.stt('dve', h1[:], xt[:], st[:, 2:3], A_[which][:], ALU.mult, ALU.mult, [xt.res, st.res, A_[which].res], [h1.res])
                O.tt('pool', hb[:], h1[:], Sh_[which][:], ALU.add, [h1.res, Sh_[which].res], [hb.res])
                for k in range(8):
                    O.tr(pT[:, k, :], hb[:, k * 128:(k + 1) * 128], ident_b[:], [hb.res, ident_b.res], [pT.res])
                hT = hT_r.next()
                O.acopy(hT[:], pT[:], [pT.res], [hT.res])

                ropem = None if rp is None else (rp, 0, 16)
                roped = None if rp is None else (rp, 32, 64)

                def F_qk(c0, gain, rope):
                    ps = proj(hT, c0, c0 + 512)
                    tg = headnorm(ps[:, :], ps.res, 8, 64, gain)
                    dst = dst_r.next()
                    rope_into(v3(tg[:, :512], 8), tg.res, v3(dst[:, :512], 8), dst.res, 8, 0, 64, rope)
                    return dst

                def F_cq():
                    ps = proj(hT, 1536, 1920)
                    tg = headnorm(ps[:, 0:384], ps.res, 1, 384, gains['mla_cq_g'])
                    dst = dst_r.next()
                    O.acopy(dst[:, :384], tg[:, :384], [tg.res], [dst.res])
                    return dst

                def B_cq(dst):
                    pq = ptq_r.next()
                    for c in range(3):
                        O.tr(pq[:, c, :], dst[:, c * 128:(c + 1) * 128], ident_b[:], [dst.res, ident_b.res], [pq.res])
                    cT = cT_r.next()
                    O.acopy(cT[:], pq[:, 0:3, :], [pq.res], [cT.res])
                    dstq = dst_r.next()
                    for g in range(2):
                        ps = pj_r.next()
                        for k in range(3):
                            O.mm(ps[:, 0:384], cT[:, k, :], wuq[:, k, g * 384:(g + 1) * 384], k == 0, k == 2, [cT.res, wuq.res], [ps.res])
                        tg = headnorm(ps[:, 0:384], ps.res, 4, 96, gains['mla_q_g'])
                        rope_into(v3(tg[:, :384], 4), tg.res, v3(dstq[:, g * 384:(g + 1) * 384], 4), dstq.res, 4, 64, 32, ropem)
                    return dstq

                def F_ckv():
                    pskv = proj(hT, 1920, 2208)
                    tg = headnorm(pskv[:, 0:256], pskv.res, 1, 256, gains['mla_ckv_g'])
                    dst = dst_r.next()
                    O.acopy(dst[:, :256], tg[:, :256], [tg.res], [dst.res])
                    krs = krs_r.next()
                    O.acopy(krs[:], pskv[:, 256:288], [pskv.res], [krs.res])
                    return dst, krs

                def B_ckv(dst, krs):
                    pq = ptq_r.next()
                    for c in range(2):
                        O.tr(pq[:, c, :], dst[:, c * 128:(c + 1) * 128], ident_b[:], [dst.res, ident_b.res], [pq.res])
                    cT = cT_r.next()
                    O.acopy(cT[:, 0:2, :], pq[:, 0:2, :], [pq.res], [cT.res])
                    O.copy('pool', mkraw[:, :, 64:96], krs[:].unsqueeze(1).to_broadcast([128, 8, 32]), [krs.res, mkraw.res], [mkraw.res])
                    vs = vst_r.next()
                    for g in range(2):
                        ps = pj_r.next()
                        for k in range(2):
                            O.mm(ps[:, :], cT[:, k, :], wukv[:, k, g * 512:(g + 1) * 512], k == 0, k == 1, [cT.res, wukv.res], [ps.res])
                        p3 = v3(ps[:, :], 4)
                        O.acopy(mkraw[:, 4 * g:4 * g + 4, 0:64], p3[:, :, 0:64], [ps.res, mkraw.res], [mkraw.res])
                        O.acopy(vs[:, 4 * g:4 * g + 4, 0:64], p3[:, :, 64:128], [ps.res, vs.res], [vs.res])
                    P.dma('sp', scr['V_m'].rearrange("h p t d -> p h t d")[:, :, kt, :], vs[:], [vs.res], [], key=vs.res)
                    tg = headnorm(mkraw[:].rearrange("p h d -> p (h d)"), mkraw.res, 8, 96, gains['mla_k_g'])
                    dstk = dst_r.next()
                    rope_into(v3(tg[:, :768], 8), tg.res, v3(dstk[:, :768], 8), dstk.res, 8, 64, 32, ropem)
                    return dstk

                def run_carry():
                    while carry:
                        carry.pop(0)()

                d_naq = F_qk(0, gains['na_q_g'], None) if full else None
                d_nak = F_qk(512, gains['na_k_g'], None)
                run_carry()
                if full:
                    tstore4(d_naq, scr['QT_na'], qt * 128)
                ps = proj(hT, 1024, 1536)
                vs = vst_r.next()
                O.acopy(vs[:, :, 0:64], v3(ps[:, :], 8), [ps.res, vs.res], [vs.res])
                P.dma('sp', scr['V_na'].rearrange("h p t d -> p h t d")[:, :, kt, :], vs[:], [vs.res], [], key=vs.res)
                tstore4(d_nak, scr['KT_na'], kt * 128)
                d_cq = F_cq() if full else None
                d_ckv, krs = F_ckv()
                dstq = B_cq(d_cq) if full else None
                d_dq = F_qk(2208, gains['diff_q_g'], roped) if full else None
                dstk = B_ckv(d_ckv, krs)
                d_dk = F_qk(2720, gains['diff_k_g'], roped)
                if full:
                    tstore8(dstq, scr['QT_m'], qt * 128)
                ps = proj(hT, 3232, 3744)
                vsd = vstd_r.next()
                pd3 = v3(ps[:, :], 4)
                O.acopy(vsd[:, :, 0:64], pd3[:, :, 0:64], [ps.res, vsd.res], [vsd.res])
                O.acopy(vsd[:, :, 66:130], pd3[:, :, 64:128], [ps.res, vsd.res], [vsd.res])
                P.dma('sp', scr['V_d'].rearrange("h p t d -> p h t d")[:, :, kt, :], vsd[:], [vsd.res], [], key=vsd.res)
                if full:
                    carry.append(lambda d=d_dq, q=qt: tstore4(d, scr['QT_d'], q * 128))
                carry.append(lambda d=dstk, k_=kt: tstore8(d, scr['KT_m'], k_ * 128))
                carry.append(lambda d=d_dk, k_=kt: tstore4(d, scr['KT_d'], k_ * 128))

            for i in range(min(2, DBG1_TILES)):
                do_tile(ctxin[i * 128:(i + 1) * 128, :], 1, None, not last, i, i)
            for i in range(min(NLT, max(0, DBG1_TILES - 2))):
                do_tile(xown[i * 128:(i + 1) * 128, :], 0, C['rope_own'][i * 128:(i + 1) * 128, :], True, 2 + i, 2 + i)
            for i in range(min(NLT, max(0, DBG1_TILES - 34))):
                do_tile(xother[i * 128:(i + 1) * 128, :], 0, C['rope_other'][i * 128:(i + 1) * 128, :], both, 2 + NLT + i, 2 + NLT + i)
            while carry:
                carry.pop(0)()
        P.barrier()
        if DEBUG_STOP == 1:
            return

        with ExitStack() as ph:
            KT_r = sring(ph, 'KT', 2, [128, NK], BF16)
            QT_r = sring(ph, 'QT', 2, [128, NQL], BF16)
            V_r = sring(ph, 'V', 2, [128, NKT * 132], BF16)
            PT_r = sring(ph, 'PT', 4, [128, 1024], BF16)
            oT_r = sring(ph, 'oT', 2, [128, 512], F32)
            om_r = sring(ph, 'om', 2, [128, 4, 128], F32)
            ost_r = sring(ph, 'ost', 3, [128, 4, 128], BF16)
            rc_r = sring(ph, 'rc', 4, [128, 8], F32)
            dt_r = sring(ph, 'dt', 2, [128, 128], F32)
            dsq = sbt(ph, 'dsq', [128, 128], F32)
            NCH = 4
            TPC = (NTBL + NCH - 1) // NCH
            nabf_r = sring(ph, 'nab_f', 2, [128, TPC, 128], F32)
            nam_b = sbt(ph, 'nam_b', [128, NTBL, 128], BF16)
            nab_b = sbt(ph, 'nab_b', [128, NTBL, 128], BF16)
            P.dma('pool', nam_b[:], C['namask'], [], [nam_b.res], key='wl4')

            ps_s = pring(ph, 'ps_s', 3, [128, 512], F32)
            accT_r = pring(ph, 'accT', 2, [128, 512], F32)
            ps_o = pring(ph, 'ptr', 2, [128, 512], F32)

            def vview(V, dvp):
                return V[:, 0:NKT * (dvp + 1)].rearrange("p (t d) -> p t d", d=dvp + 1)

            def load_head(KT_src, QT_src, V_src, nrow, dvp):
                KT = KT_r.next(); QT = QT_r.next(); V = V_r.next()
                P.dma('sp', KT[0:nrow, :], KT_src, [], [KT.res], key=KT.res)
                P.dma('act', QT[0:nrow, 0:NQL], QT_src[:, 0:NQL], [], [QT.res], key=QT.res)
                P.dma('sp', vview(V, dvp), V_src, [], [V.res], key=V.res)
                return KT, QT, V

            def attend_T(KT, QT, V, p0, nrow, q0, nq, kts, scale, dvp, groups):
                Vv = vview(V, dvp)
                nqs = nq // 128
                nk = len(kts)
                accs = [accT_r.next() for _ in groups]
                pend = []

                def pvT(item):
                    idx, kt, pt = item
                    for gi, c0 in enumerate(groups):
                        O.mm(accs[gi][0:65, 0:nq], Vv[:, kt, c0:c0 + 65], pt[:, 0:nq], idx == 0, idx == nk - 1,
                             [pt.res, V.res], [accs[gi].res])
                for idx, kt in enumerate(kts):
                    ps = ps_s.next()
                    O.mm(ps[:, 0:nq], KT[p0:p0 + nrow, kt * 128:(kt + 1) * 128], QT[p0:p0 + nrow, q0:q0 + nq], True, True,
                         [KT.res, QT.res], [ps.res])
                    pt = PT_r.next()
                    O.act(pt[:, 0:nq], ps[:, 0:nq], AF.Exp, [ps.res], [pt.res], scale=scale)
                    pend.append((idx, kt, pt))
                    if len(pend) > 2:
                        pvT(pend.pop(0))
                while pend:
                    pvT(pend.pop(0))
                outs = []
                for gi in range(len(groups)):
                    oT = oT_r.next()
                    O.copy('dve', oT[0:65, 0:nq], accs[gi][0:65, 0:nq], [accs[gi].res], [oT.res])
                    ptr = ps_o.next()
                    for qs in range(nqs):
                        O.tr(ptr[:, qs * 128:qs * 128 + 65], oT[0:65, qs * 128:(qs + 1) * 128], ident_f[0:65, 0:65],
                             [oT.res, ident_f.res], [ptr.res])
                    outs.append(ptr)
                return outs

            def finalize_simple(accs, nqs, dv, ocol, q0):
                ost = ost_r.next()
                rc = rc_r.next()
                for qs in range(nqs):
                    acc, off = accs[qs]
                    O.recip(rc[:, qs:qs + 1], acc[:, off + dv:off + dv + 1], [acc.res, rc.res], [rc.res])
                    O.ts('dve', ost[:, qs, 0:dv], acc[:, off:off + dv], rc[:, qs:qs + 1], ALU.mult, [acc.res, rc.res, ost.res], [ost.res])
                P.dma('sp', scr['O'][q0:q0 + nqs * 128, :].rearrange("(s p) n -> p s n", p=128)[:, :, ocol:ocol + dv],
                      ost[:, 0:nqs, 0:dv], [ost.res], [], key=ost.res)

            all_kts = list(range(NKT))
            for pair in range(4 if DBG2 >= 1 else 0):
                KT = KT_r.next(); QT = QT_r.next()
                P.dma('sp', KT[:, :], scr['KT_na'][pair * 128:(pair + 1) * 128, :], [], [KT.res], key=KT.res)
                P.dma('act', QT[:, 0:NQL], scr['QT_na'][pair * 128:(pair + 1) * 128, 0:NQL], [], [QT.res], key=QT.res)
                for hh in range(2):
                    h = pair * 2 + hh
                    p0 = hh * 64
                    V = V_r.next()
                    Vv = vview(V, 65)
                    P.dma('sp', Vv, scr['V_na'][h], [], [V.res], key=V.res)
                    for c_ in range(NCH):
                        t0_, t1_ = c_ * TPC, min(NTBL, (c_ + 1) * TPC)
                        nf = nabf_r.next()
                        P.dma('sp', nf[:, 0:t1_ - t0_, :], W['nab'][h][:, t0_:t1_, :], [], [nf.res], key=nf.res)
                        O.stt('dve', nab_b[:, t0_:t1_, :], nf[:, 0:t1_ - t0_, :], 1.0 / NA_SCALE, nam_b[:, t0_:t1_, :], ALU.mult, ALU.add,
                              [nf.res, nam_b.res, nab_b.res], [nab_b.res])
                    if not last:
                        (ptr,) = attend_T(KT, QT, V, p0, 64, 0, 256, [0, 1], NA_SCALE, 65, [0])
                        finalize_simple([(ptr, 0), (ptr, 128)], 2, 64, h * 64, 0)
                    qtiles = [('own', i_) for i_ in range(NLT)] + ([('other', i_) for i_ in range(NLT)] if both else [])
                    for grp in range(len(qtiles) // 4):
                        acc = ps_o.next()
                        accs = [(acc, 128 * j) for j in range(4)]
                        for j in range(4):
                            kind, li = qtiles[grp * 4 + j]
                            b_self, b_cross, tb0 = (2, 2 + NLT, 5) if kind == 'own' else (2 + NLT, 2, 29)
                            if 2 <= li <= NLT - 3:
                                slots = [(b_self + li + d, d + 2) for d in range(-2, 3)]
                            elif li < 2:
                                slots = [(b_self + s, tb0 + li * 6 + s) for s in range(4)] + \
                                        [(b_cross + NLT - 2 + s, tb0 + li * 6 + 4 + s) for s in range(2)]
                            else:
                                e_ = li - (NLT - 2) + 2
                                slots = [(b_self + NLT - 4 + s, tb0 + e_ * 6 + s) for s in range(4)] + \
                                        [(b_cross + s, tb0 + e_ * 6 + 4 + s) for s in range(2)]
                            qti = li if kind == 'own' else NLT + li
                            q0 = CTX + qti * 128
                            kts = [0, 1] + [s_[0] for s_ in slots]
                            nsl = len(kts)
                            pss = [ps_s.next(), ps_s.next()]
                            for si, kt in enumerate(kts):
                                ps = pss[si // 4]
                                c0 = (si % 4) * 128
                                tb = None if si < 2 else slots[si - 2][1]
                                O.mm(ps[:, c0:c0 + 128], KT[p0:p0 + 64, kt * 128:(kt + 1) * 128], QT[p0:p0 + 64, q0:q0 + 128],
                                     True, tb is None, [KT.res, QT.res], [ps.res])
                                if tb is not None:
                                    O.mm(ps[:, c0:c0 + 128], ident_b[:], nab_b[:, tb, :], False, True, [ident_b.res, nab_b.res], [ps.res])
                            pt = PT_r.next()
                            O.act(pt[:, 0:512], pss[0][:, :], AF.Exp, [pss[0].res], [pt.res], scale=NA_SCALE)
                            nb = (nsl - 4) * 128
                            O.act(pt[:, 512:512 + nb], pss[1][:, 0:nb], AF.Exp, [pss[1].res, pt.res], [pt.res], scale=NA_SCALE)
                            for si, kt in enumerate(kts):
                                O.mm(acc[:, 128 * j:128 * j + 65], pt[:, si * 128:(si + 1) * 128], Vv[:, kt, 0:65], si == 0, si == nsl - 1,
                                     [pt.res, V.res], [acc.res])
                        finalize_simple(accs, 4, 64, h * 64, CTX + grp * 512)
            for h in range(8 if DBG2 >= 2 else 0):
                KT, QT, V = load_head(scr['KT_m'][h], scr['QT_m'][h], scr['V_m'][h], 96, 65)
                if not last:
                    (ptr,) = attend_T(KT, QT, V, 0, 96, 0, 256, [0, 1], MLA_SCALE, 65, [0])
                    finalize_simple([(ptr, 0), (ptr, 128)], 2, 64, 512 + h * 64, 0)
                for ch in range((NQL - CTX) // 512):
                    (ptr,) = attend_T(KT, QT, V, 0, 96, CTX + ch * 512, 512, all_kts, MLA_SCALE, 65, [0])
                    finalize_simple([(ptr, 128 * j) for j in range(4)], 4, 64, 512 + h * 64, CTX + ch * 512)

            def finalize_diff(om0, om1, nqs, h, q0):
                ost = ost_r.next()
                sgb = gains['diff_subln_g']
                for qs in range(nqs):
                    dt = dt_r.next()
                    rc = rc_r.next()
                    O.stt('dve', dt[:], om1[:, qs, :], nlam_b[:], om0[:, qs, :], ALU.mult, ALU.add, [om0.res, om1.res, nlam_b.res], [dt.res])
                    O.tt('dve', dsq[:], dt[:], dt[:], ALU.mult, [dt.res], [dsq.res])
                    O.red('dve', rc[:, 3:4], dsq[:], [dsq.res, rc.res], [rc.res])
                    O.act(rc[:, 4:5], rc[:, 3:4], AF.Ln, [rc.res, eps_t.res], [rc.res], scale=1.0 / 128, bias=eps_t[:])
                    O.act(rc[:, 5:6], rc[:, 4:5], AF.Exp, [rc.res], [rc.res], scale=-0.5)
                    O.stt('dve', ost[:, qs, :], dt[:], rc[:, 5:6], sgb[:], ALU.mult, ALU.mult, [dt.res, rc.res, sgb.res, ost.res], [ost.res])
                P.dma('sp', scr['O'][q0:q0 + nqs * 128, :].rearrange("(s p) n -> p s n", p=128)[:, :, 1024 + h * 128:1024 + (h + 1) * 128],
                      ost[:, 0:nqs, :], [ost.res], [], key=ost.res)

            for h in range(4 if DBG2 >= 3 else 0):
                KT = KT_r.next(); V = V_r.next()
                P.dma('sp', KT[:, :], scr['KT_d'][h * 128:(h + 1) * 128, :], [], [KT.res], key=KT.res)
                P.dma('sp', vview(V, 131), scr['V_d'][h], [], [V.res], key=V.res)
                QTm = []
                for m in range(2):
                    QTz = QT_r.next()
                    P.dma('act', QTz[:, 0:NQL], scr['QT_d'][h * 128:(h + 1) * 128, 0:NQL], [], [QTz.res], key=QTz.res)
                    O.memset('pool', QTz[(1 - m) * 64:(1 - m) * 64 + 64, 0:NQL], 0.0, [QTz.res])
                    QTm.append(QTz)
                chunks = []
                if not last:
                    chunks.append((0, 256, [0, 1]))
                for ch in range((NQL - CTX) // 512):
                    chunks.append((CTX + ch * 512, 512, all_kts))
                for (q0, nq, kts) in chunks:
                    nqs = nq // 128
                    oms = []
                    for m in range(2):
                        ptrs = attend_T(KT, QTm[m], V, 0, 128, q0, nq, kts, DIFF_SCALE, 131, [0, 66])
                        om = om_r.next()
                        for gi, ptr in enumerate(ptrs):
                            rc = rc_r.next()
                            for qs in range(nqs):
                                O.recip(rc[:, qs:qs + 1], ptr[:, qs * 128 + 64:qs * 128 + 65], [ptr.res, rc.res], [rc.res])
                                O.ts('dve', om[:, qs, gi * 64:(gi + 1) * 64], ptr[:, qs * 128:qs * 128 + 64], rc[:, qs:qs + 1], ALU.mult,
                                     [ptr.res, rc.res, om.res], [om.res])
                        oms.append(om)
                    finalize_diff(oms[0], oms[1], nqs, h, q0)
        P.barrier()
        if DEBUG_STOP == 2:
            return

        with ExitStack() as ph:
            NZ = 4608
            wz = sbt(ph, 'wz', [128, 8, NZ], BF16)
            wsrc = W['w_in'].rearrange("(k p) n -> p k n", p=128)
            for k in range(8):
                P.dma('pool', wz[:, k, :], wsrc[:, k, 3744:D_IN], [], [wz.res + str(k)], key='wl%d' % k)
            wz_res = [wz.res + str(k) for k in range(8)]
            wbr = sbt(ph, 'wbr', [128, 12, 1024], BF16)
            for b_ in range(3):
                P.dma('pool', wbr[:, 4 * b_:4 * b_ + 4, :], W['w_br'][b_].rearrange("(k p) n -> p k n", p=128), [], [wbr.res + str(b_)],
                      key='wl%d' % b_)
            wout = sbt(ph, 'wout', [128, 8, 1024], BF16)
            P.dma('pool', wout[:], W['w_out'].rearrange("(k p) n -> p k n", p=128), [], [wout.res], key='wl3')

            xt_r = sring(ph, 'xt3', 2, [128, 1024], F32)
            o_r = sring(ph, 'o3', 2, [128, 1536], BF16)
            st_r = sring(ph, 'st3', 2, [128, 4], F32)
            et_r = sring(ph, 'et', 2, [128, 512], F32)
            hb = sbt(ph, 'hb3', [128, 1024], BF16)
            hT = sbt(ph, 'hT3', [128, 8, 128], BF16)
            zs = sbt(ph, 'zs', [128, 1536], F32)
            zg = sbt(ph, 'zg', [128, 1536], BF16)
            sgm = sbt(ph, 'sgm', [128, 3072], BF16)
            og = sbt(ph, 'og', [128, 1536], BF16)
            ogT = sbt(ph, 'ogT', [128, 12, 128], BF16)
            yacc = sbt(ph, 'yacc', [128, 1024], F32)
            ytmp = sbt(ph, 'ytmp', [128, 1024], F32)
            yb = sbt(ph, 'yb', [128, 1024], BF16)
            yT = sbt(ph, 'yT', [128, 8, 128], BF16)
            pT = pst(ph, 'pT3', [128, 8, 128], BF16)
            pT2 = pst(ph, 'pT23', [128, 8, 128], BF16)
            pj_r = pring(ph, 'pj3', 4, [128, 512], F32)

            def out_tile(src_ap, which, qrow0, dst_ap):
                xt = xt_r.next()
                P.dma('sp', xt[:], src_ap, [], [xt.res], key=xt.res)
                ot = o_r.next()
                P.dma('act', ot[:], scr['O'][qrow0:qrow0 + 128, :], [], [ot.res], key=ot.res)
                st = st_r.next()
                O.memset('dve', st[:], 0.0, [st.res])
                O.act(yacc[:], xt[:], AF.Square, [xt.res, st.res], [yacc.res, st.res], accum_out=st[:, 0:1])
                O.act(st[:, 1:2], st[:, 0:1], AF.Ln, [st.res, eps_t.res], [st.res], scale=1.0 / 1024, bias=eps_t[:])
                O.act(st[:, 2:3], st[:, 1:2], AF.Exp, [st.res], [st.res], scale=-0.5)
                O.stt('dve', ytmp[:], xt[:], st[:, 2:3], A_[which][:], ALU.mult, ALU.mult, [xt.res, st.res, A_[which].res], [ytmp.res])
                O.tt('pool', hb[:], ytmp[:], Sh_[which][:], ALU.add, [ytmp.res, Sh_[which].res], [hb.res])
                for k in range(8):
                    O.tr(pT[:, k, :], hb[:, k * 128:(k + 1) * 128], ident_b[:], [hb.res, ident_b.res], [pT.res])
                O.acopy(hT[:], pT[:], [pT.res], [hT.res])
                if DBG3 < 2:
                    return
                for g in range(9):
                    ps = pj_r.next()
                    for k in range(8):
                        O.mm(ps[:, :], hT[:, k, :], wz[:, k, g * 512:(g + 1) * 512], k == 0, k == 7, [hT.res, wz_res[k]], [ps.res])
                    et = et_r.next()
                    O.act(et[:], ps[:, :], AF.Exp, [ps.res], [et.res], scale=-1.0)
                    O.ts('dve', et[:], et[:], 1.0, ALU.add, [et.res], [et.res])
                    if g < 3:
                        O.recip(zg[:, g * 512:(g + 1) * 512], et[:], [et.res, zg.res], [zg.res])
                        O.copy('dve', zs[:, g * 512:(g + 1) * 512], ps[:, :], [ps.res, zs.res], [zs.res])
                    else:
                        O.recip(sgm[:, (g - 3) * 512:(g - 2) * 512], et[:], [et.res, sgm.res], [sgm.res])
                if DBG3 < 3:
                    return
                O.tt('pool', zs[:], zs[:], zg[:], ALU.mult, [zs.res, zg.res], [zs.res])
                O.tt('dve', og[:], zs[:], ot[:], ALU.mult, [zs.res, ot.res], [og.res])
                if DBG3 < 4:
                    return
                for c in range(12):
                    pp = pT if c < 8 else pT2
                    O.tr(pp[:, c % 8, :], og[:, c * 128:(c + 1) * 128], ident_b[:], [og.res, ident_b.res], [pp.res])
                O.acopy(ogT[:, 0:8, :], pT[:], [pT.res, ogT.res], [ogT.res])
                O.copy('dve', ogT[:, 8:12, :], pT2[:, 0:4, :], [pT2.res, ogT.res], [ogT.res])
                if DBG3 < 5:
                    return
                for b_ in range(3):
                    for g in range(2):
                        ps = pj_r.next()
                        for k in range(4):
                            O.mm(ps[:, :], ogT[:, 4 * b_ + k, :], wbr[:, 4 * b_ + k, g * 512:(g + 1) * 512], k == 0, k == 3,
                                 [ogT.res, wbr.res + str(b_)], [ps.res])
                        gsl = sgm[:, b_ * 1024 + g * 512:b_ * 1024 + (g + 1) * 512]
                        ysl = yacc[:, g * 512:(g + 1) * 512]
                        tsl = ytmp[:, g * 512:(g + 1) * 512]
                        if b_ == 0:
                            O.tt('dve', ysl, ps[:, :], gsl, ALU.mult, [ps.res, sgm.res, yacc.res], [yacc.res])
                        else:
                            O.tt('dve', tsl, ps[:, :], gsl, ALU.mult, [ps.res, sgm.res, ytmp.res], [ytmp.res])
                            if b_ == 1:
                                O.tt('pool', ysl, ysl, tsl, ALU.add, [ytmp.res, yacc.res], [yacc.res])
                            else:
                                O.tt('pool', yb[:, g * 512:(g + 1) * 512], ysl, tsl, ALU.add, [ytmp.res, yacc.res, yb.res], [yb.res])
                if DBG3 < 6:
                    return
                for k in range(8):
                    O.tr(pT[:, k, :], yb[:, k * 128:(k + 1) * 128], ident_b[:], [yb.res, ident_b.res], [pT.res])
                O.acopy(yT[:], pT[:], [pT.res], [yT.res])
                for g in range(2):
                    ps = pj_r.next()
                    for k in range(8):
                        O.mm(ps[:, :], yT[:, k, :], wout[:, k, g * 512:(g + 1) * 512], k == 0, k == 7, [yT.res, wout.res], [ps.res])
                    O.tt('dve', ytmp[:, g * 512:(g + 1) * 512], ps[:, :], Gt_[which][:, g * 512:(g + 1) * 512], ALU.mult,
                         [ps.res, Gt_[which].res, ytmp.res], [ytmp.res])
                if DBG3 < 7:
                    return
                O.tt('pool', xt[:], xt[:], ytmp[:], ALU.add, [xt.res, ytmp.res], [xt.res])
                P.dma('sp', dst_ap, xt[:], [xt.res], [], key=xt.res + 'o')

            if not last:
                for i in range(min(2, DBG3_TILES)):
                    out_tile(ctxin[i * 128:(i + 1) * 128, :], 1, i * 128, ctxout[i * 128:(i + 1) * 128, :])
            for i in range(min(NLT, max(0, DBG3_TILES - 2))):
                out_tile(xown[i * 128:(i + 1) * 128, :], 0, CTX + i * 128, xout[i * 128:(i + 1) * 128, :])
            if both:
                for i in range(NLT):
                    out_tile(xother[i * 128:(i + 1) * 128, :], 0, CTX + (NLT + i) * 128, xout_other[i * 128:(i + 1) * 128, :])
        P.barrier()


def declare_scratch(nc, sfx):
    s = {}
    def d(name, shape):
        s[name] = nc.dram_tensor('scr_' + name + sfx, shape, BF16, kind=("ExternalOutput" if DEBUG_SCR else "Internal")).ap()
    d('QT_na', [512, NQ2]); d('KT_na', [512, NK]); d('V_na', [8, 128, NKT, 66])
    d('QT_m', [8, 96, NQ2]); d('KT_m', [8, 96, NK]); d('V_m', [8, 128, NKT, 66])
    d('QT_d', [512, NQ2]); d('KT_d', [512, NK]); d('V_d', [4, 128, NKT, 132])
    d('O', [NQ2, 1536])
    return s


def build_fused():
    nc = bass.Bass("TRN2", target_bir_lowering=False)
    C = {}
    for name, shape in (('cvec', [1024]), ('cctx', [1024]), ('rope_own', [HALF, 96]), ('rope_other', [HALF, 96]),
                        ('namask', [128, NTBL, 128]), ('ident', [128, 128])):
        C[name] = nc.dram_tensor(name, shape, F32, kind="ExternalInput").ap()
    xown = nc.dram_tensor('xown', [HALF, 1024], F32, kind="ExternalInput").ap()
    xother = nc.dram_tensor('xother', [HALF, 1024], F32, kind="ExternalInput").ap()
    ctxin = nc.dram_tensor('ctxin', [CTX, 1024], F32, kind="ExternalInput").ap()
    xout = nc.dram_tensor('xout', [HALF, 1024], F32, kind="ExternalOutput").ap()
    x1own = nc.dram_tensor('x1own', [HALF, 1024], F32, kind="Internal").ap()
    x1other = nc.dram_tensor('x1other', [HALF, 1024], F32, kind="Internal").ap()
    ctx1 = nc.dram_tensor('ctx1', [CTX, 1024], F32, kind="Internal").ap()
    W0 = declare_layer_weights(nc, '_0')
    W1 = declare_layer_weights(nc, '_1')
    scr = declare_scratch(nc, '')
    P = Prog(nc)
    with ExitStack() as st:
        emit_layer(nc, P, 0, False, W0, C, xown, xother, ctxin, x1own, ctx1, scr, both=True, xout_other=x1other)
        emit_layer(nc, P, 1, True, W1, C, x1own, x1other, ctx1, xout, None, scr)
        P.emit(st)
    return nc, P


def rope_table():
    t = np.arange(SEQ)
    row = (t // GRID_W).astype(np.float32)
    col = (t % GRID_W).astype(np.float32)
    out = np.zeros((SEQ, 96), np.float32)
    for rot, off in ((32, 0), (64, 32)):
        nf = rot // 4
        inv = (10000.0 ** (-np.arange(nf, dtype=np.float32) / nf)).astype(np.float32)
        ang = np.concatenate([row[:, None] * inv, col[:, None] * inv], axis=-1).astype(np.float32)
        hh = rot // 2
        out[:, off:off + hh] = np.cos(ang)
        out[:, off + hh:off + 2 * hh] = np.sin(ang)
    return out


def na_tables(half):
    def table(i, j):
        kr = np.arange(128) // 64; kc = np.arange(128) % 64
        qr = np.arange(128) // 64; qc = np.arange(128) % 64
        r = (2 * i + qr)[None, :]
        krow = (2 * j + kr)[:, None]
        rs = np.clip(r - 4, 0, 120)
        vrow = (krow >= rs) & (krow <= rs + 7)
        cstart = np.clip(qc - 8, 0, 48)[None, :]
        vcol = (kc[:, None] >= cstart) & (kc[:, None] < cstart + 16)
        dr = np.clip(krow - r + 7, 0, 14)
        dc = np.clip(kc[:, None] - qc[None, :], -15, 15) + 15
        valid = vrow & vcol & (j >= 0) & (j < 64)
        return np.broadcast_to(dr, (128, 128)), dc, valid
    tabs = []
    for d in range(-2, 3):
        tabs.append(table(10, 10 + d))
    own_g = lambda li: half * NLT + li
    oth_g = lambda oi: (1 - half) * NLT + oi
    for self_g, cross_g in ((own_g, oth_g), (oth_g, own_g)):
        for li in (0, 1, NLT - 2, NLT - 1):
            i = self_g(li)
            if li < 2:
                keys = [self_g(s_) for s_ in range(4)] + [cross_g(NLT - 2 + s_) for s_ in range(2)]
            else:
                keys = [self_g(NLT - 4 + s_) for s_ in range(4)] + [cross_g(s_) for s_ in range(2)]
            for j in keys:
                tabs.append(table(i, j))
    dr = np.stack([t[0] for t in tabs]); dc = np.stack([t[1] for t in tabs]); va = np.stack([t[2] for t in tabs])
    return dr, dc, va


_CACHE = {}


def _get_prog():
    if 'f' not in _CACHE:
        _CACHE['f'] = build_fused()
    return _CACHE['f'][0]


def kernel(x, c, ctx, c_ctx, norm_g, w_ada, b_ada, w_in, na_rpb, na_q_g, na_k_g, mla_cq_g, mla_ckv_g, w_uq, w_ukv,
           mla_q_g, mla_k_g, diff_q_g, diff_k_g, diff_lq1, diff_lk1, diff_lq2, diff_lk2, diff_subln_g, w_br, w_out):
    f = lambda a: np.ascontiguousarray(np.asarray(a, dtype=np.float32))
    x = f(x); c = f(c); ctx = f(ctx); c_ctx = f(c_ctx)
    na_rpb = f(na_rpb)
    rt = rope_table()
    ident = np.eye(128, dtype=np.float32)
    wmaps = {}
    for l in range(DEPTH):
        p = {
            'norm_g': f(norm_g[l]), 'w_ada': f(w_ada[l]), 'b_ada': f(b_ada[l]), 'w_in': f(w_in[l]),
            'na_q_g': f(na_q_g[l]), 'na_k_g': f(na_k_g[l]), 'mla_cq_g': f(mla_cq_g[l]), 'mla_ckv_g': f(mla_ckv_g[l]),
            'w_uq': f(w_uq[l]), 'w_ukv': f(w_ukv[l]), 'mla_q_g': f(mla_q_g[l]), 'mla_k_g': f(mla_k_g[l]),
            'diff_q_g': f(diff_q_g[l]), 'diff_k_g': f(diff_k_g[l]),
            'diff_l': np.ascontiguousarray(np.stack([f(diff_lq1[l]), f(diff_lk1[l]), f(diff_lq2[l]), f(diff_lk2[l])])),
            'diff_subln_g': f(diff_subln_g[l]), 'w_br': f(w_br[l]), 'w_out': f(w_out[l]),
        }
        for k_, v in p.items():
            wmaps['%s_%d' % (k_, l)] = v
    per_half = []
    for half in range(2):
        dr, dc, va = na_tables(half)
        mask = np.where(va, 0.0, MASKVAL).astype(np.float32)
        d = {'rope_own': np.ascontiguousarray(rt[half * HALF:(half + 1) * HALF]),
             'rope_other': np.ascontiguousarray(rt[(1 - half) * HALF:(2 - half) * HALF]),
             'namask': np.ascontiguousarray(mask.transpose(1, 0, 2))}
        for l in range(DEPTH):
            g = na_rpb[l][:, dr, dc]
            g = np.where(va[None], g, np.float32(0.0))
            d['nab_%d' % l] = np.ascontiguousarray(g.transpose(0, 2, 1, 3))
        per_half.append(d)
    maps = []
    for core in range(8):
        b, half = core // 2, core % 2
        m = {'xown': np.ascontiguousarray(x[b, half * HALF:(half + 1) * HALF]),
             'xother': np.ascontiguousarray(x[b, (1 - half) * HALF:(2 - half) * HALF]),
             'ctxin': np.ascontiguousarray(ctx[b]), 'cvec': np.ascontiguousarray(c[b]), 'cctx': c_ctx, 'ident': ident}
        m.update(per_half[half])
        m.update(wmaps)
        maps.append(m)
    nc = _get_prog()
    res = run_bass_kernel_spmd(nc, maps, core_ids=list(range(8)))
    out = np.empty_like(x)
    for core in range(8):
        b, half = core // 2, core % 2
        out[b, half * HALF:(half + 1) * HALF] = res.results[core]['xout']
    return out
```

```python
import math
from contextlib import ExitStack

import numpy as np
import concourse.bass as bass
import concourse.mybir as mybir
from concourse.bass_utils import run_bass_kernel_spmd

F32 = mybir.dt.float32
BF16 = mybir.dt.bfloat16
AF = mybir.ActivationFunctionType
ALU = mybir.AluOpType
AX = mybir.AxisListType

D_MODEL = 1024
BATCH = 4
SEQ = 8192
DEPTH = 2
GRID_W = 64
CTX = 256
EPS = 1e-6
D_IN = 8352
NA_SCALE = 64 ** -0.5
MLA_SCALE = 96 ** -0.5
DIFF_SCALE = 64 ** -0.5
HALF = SEQ // 2
NQ = CTX + HALF
NQ2 = CTX + SEQ
NK = CTX + SEQ
NKT = NK // 128
NLT = HALF // 128
NTBL = 5 + 24 + 24
MASKVAL = -240000.0

ENGS = ['pe', 'act', 'dve', 'pool', 'sp']
DEBUG_STOP = None
DEBUG_SCR = False
DBG0 = 99
DBG1 = 99
DBG1_TILES = 99
DBG2 = 99
DBG3 = 99
DBG3_TILES = 99
DBG3X = 99


class Op:
    __slots__ = ('eng', 'fn', 'deps', 'is_dma', 'key', 'signal', 'sig_idx', 'dma_val')

    def __init__(self, eng, fn, is_dma, key):
        self.eng = eng
        self.fn = fn
        self.deps = []
        self.is_dma = is_dma
        self.key = key
        self.signal = is_dma
        self.sig_idx = None
        self.dma_val = None


class Prog:
    def __init__(self, nc):
        self.nc = nc
        self.q = {e: [] for e in ENGS}
        self.last_w = {}
        self.readers = {}
        self.dma_count = {}
        self.dma_last = {}
        self.nops = 0

    def _dep(self, op, prod):
        if prod is None or prod is op:
            return
        if (not prod.is_dma) and (not op.is_dma) and prod.eng == op.eng and op.eng in ('pe', 'sp'):
            return
        if prod not in op.deps:
            op.deps.append(prod)
            prod.signal = True

    def add(self, eng, fn, reads=(), writes=(), dma=False, key=None):
        op = Op(eng, fn, dma, key)
        for r in reads:
            self._dep(op, self.last_w.get(r))
        for w in writes:
            self._dep(op, self.last_w.get(w))
            for rd in self.readers.get(w, ()):
                self._dep(op, rd)
        for r in reads:
            self.readers.setdefault(r, []).append(op)
        for w in writes:
            self.last_w[w] = op
            self.readers[w] = []
        if dma:
            prev = self.dma_last.get(key)
            if prev is not None:
                self._dep(op, prev)
            self.dma_last[key] = op
            n = self.dma_count.get(key, 0) + 1
            self.dma_count[key] = n
            op.dma_val = 16 * n
        self.q[eng].append(op)
        self.nops += 1
        return op

    def dma(self, eng, out, in_, reads, writes, key):
        return self.add(eng, lambda e: e.dma_start(out=out, in_=in_), reads, writes, dma=True, key=key)

    def barrier(self):
        lasts = [self.q[e][-1] for e in ENGS if self.q[e]]
        lasts = [p for p in lasts if p.fn is not None]
        dmas = list(self.dma_last.values())
        for e in ENGS:
            op = Op(e, None, False, None)
            for p in lasts + dmas:
                if p.is_dma or p.eng != e:
                    if p not in op.deps:
                        op.deps.append(p)
                        p.signal = True
            self.q[e].append(op)
        self.last_w = {}
        self.readers = {}

    def finish(self, res):
        op = Op('sp', None, False, None)
        for r in res:
            p = self.last_w.get(r)
            if p is not None and p not in op.deps:
                op.deps.append(p)
                p.signal = True
        self.q['sp'].append(op)

    def emit(self, stack):
        nc = self.nc
        for e in ENGS:
            c = 0
            for op in self.q[e]:
                if not op.is_dma and op.signal:
                    c += 1
                    op.sig_idx = c
        esem = {e: stack.enter_context(nc.semaphore('s_' + e)) for e in ENGS if e != 'sp'}
        dsem = {k: stack.enter_context(nc.semaphore('d%d' % i)) for i, k in enumerate(self.dma_count)}
        self.n_sems = len(esem) + len(dsem)
        block = stack.enter_context(nc.Block())
        q = self.q

        def run(e, handle):
            seen = {}
            for op in q[e]:
                for p in op.deps:
                    if p.is_dma:
                        s, v = dsem[p.key], p.dma_val
                    else:
                        s, v = esem[p.eng], p.sig_idx
                    sid = id(s)
                    if seen.get(sid, 0) >= v:
                        continue
                    seen[sid] = v
                    handle.wait_ge(s, v)
                if op.fn is None:
                    continue
                ins = op.fn(handle)
                if op.is_dma:
                    ins.then_inc(dsem[op.key], 16)
                elif op.signal:
                    ins.then_inc(esem[e], 1)

        @block.tensor
        def _(h):
            run('pe', h)

        @block.scalar
        def _(h):
            run('act', h)

        @block.vector
        def _(h):
            run('dve', h)

        @block.gpsimd
        def _(h):
            run('pool', h)

        @block.sync
        def _(h):
            run('sp', h)


class T:
    def __init__(self, ap, res):
        self.t = ap
        self.res = res

    def __getitem__(self, k):
        return self.t[k]


class Ring:
    def __init__(self, tiles):
        self.tiles = tiles
        self.i = 0

    def next(self):
        t = self.tiles[self.i % len(self.tiles)]
        self.i += 1
        return t


def declare_layer_weights(nc, sfx):
    w = {}

    def d(name, shape):
        w[name] = nc.dram_tensor(name + sfx, shape, F32, kind="ExternalInput").ap()
    d('norm_g', [1024]); d('w_ada', [1024, 3072]); d('b_ada', [3072]); d('w_in', [1024, D_IN])
    d('nab', [8, 128, NTBL, 128])
    d('na_q_g', [64]); d('na_k_g', [64]); d('mla_cq_g', [384]); d('mla_ckv_g', [256])
    d('w_uq', [384, 768]); d('w_ukv', [256, 1024]); d('mla_q_g', [96]); d('mla_k_g', [96])
    d('diff_q_g', [64]); d('diff_k_g', [64]); d('diff_l', [4, 64]); d('diff_subln_g', [128])
    d('w_br', [3, 512, 1024]); d('w_out', [1024, 1024])
    return w


class Ops:
    def __init__(self, P):
        self.P = P

    def mm(self, out, lhsT, rhs, start, stop, reads, writes, skip=False):
        self.P.add('pe', lambda e: e.matmul(out, lhsT=lhsT, rhs=rhs, start=start, stop=stop, skip_group_check=skip), reads, writes)

    def tr(self, out, in_, ident, reads, writes):
        self.P.add('pe', lambda e: e.transpose(out=out, in_=in_, identity=ident), reads, writes)

    def act(self, out, in_, func, reads, writes, **kw):
        self.P.add('act', lambda e: e.activation(out=out, in_=in_, func=func, **kw), reads, writes)

    def acopy(self, out, in_, reads, writes):
        self.P.add('act', lambda e: e.copy(out=out, in_=in_), reads, writes)

    def copy(self, eng, out, in_, reads, writes):
        self.P.add(eng, lambda e: e.tensor_copy(out=out, in_=in_), reads, writes)

    def tt(self, eng, out, in0, in1, op, reads, writes):
        self.P.add(eng, lambda e: e.tensor_tensor(out=out, in0=in0, in1=in1, op=op), reads, writes)

    def ts(self, eng, out, in0, s1, op0, reads, writes, s2=None, op1=None):
        if op1 is None:
            self.P.add(eng, lambda e: e.tensor_scalar(out=out, in0=in0, scalar1=s1, scalar2=None, op0=op0), reads, writes)
        else:
            self.P.add(eng, lambda e: e.tensor_scalar(out=out, in0=in0, scalar1=s1, scalar2=s2, op0=op0, op1=op1), reads, writes)

    def stt(self, eng, out, in0, scalar, in1, op0, op1, reads, writes):
        self.P.add(eng, lambda e: e.scalar_tensor_tensor(out=out, in0=in0, scalar=scalar, in1=in1, op0=op0, op1=op1), reads, writes)

    def red(self, eng, out, in_, reads, writes):
        self.P.add(eng, lambda e: e.tensor_reduce(out=out, in_=in_, axis=AX.X, op=ALU.add), reads, writes)

    def recip(self, out, in_, reads, writes):
        nc = self.P.nc

        def fn(e):
            with nc.allow_low_precision(reason="fp32 reciprocal rounded once to the bf16 consumer dtype"):
                return e.reciprocal(out=out, in_=in_)
        self.P.add('dve', fn, reads, writes)

    def memset(self, eng, ap, val, writes):
        self.P.add(eng, lambda e: e.memset(ap, val), [], writes)


def emit_layer(nc, P, l, last, W, C, xown, xother, ctxin, xout, ctxout, scr, both=False, xout_other=None):
    lam_init = 0.8 - 0.6 * math.exp(-0.3 * l)
    L = 'L%d_' % l
    NQL = NQ2 if both else NQ
    O = Ops(P)

    def sbt(stack, name, shape, dt):
        return T(stack.enter_context(nc.sbuf_tensor(L + name, shape, dt)), L + name)

    def pst(stack, name, shape, dt):
        return T(stack.enter_context(nc.psum_tensor(L + name, shape, dt)), L + name)

    def sring(stack, name, n, shape, dt):
        return Ring([sbt(stack, '%s%d' % (name, i), shape, dt) for i in range(n)])

    def pring(stack, name, n, shape, dt):
        return Ring([pst(stack, '%s%d' % (name, i), shape, dt) for i in range(n)])

    def v3(ap, h):
        return ap.rearrange("p (h d) -> p h d", h=h)

    with ExitStack() as LS:
        ident_f = sbt(LS, 'ident_f', [128, 128], F32)
        ident_b = sbt(LS, 'ident_b', [128, 128], BF16)
        ones_f = sbt(LS, 'ones_f', [1, 128], F32)
        eps_t = sbt(LS, 'eps_t', [128, 1], F32)
        A_ = [sbt(LS, 'A%d' % i, [128, 1024], F32) for i in range(2)]
        Sh_ = [sbt(LS, 'Sh%d' % i, [128, 1024], F32) for i in range(2)]
        Gt_ = [sbt(LS, 'Gt%d' % i, [128, 1024], F32) for i in range(2)]
        nlam_b = sbt(LS, 'nlam_b', [128, 1], F32)
        gains = {}
        for nm, n in (('na_q_g', 64), ('na_k_g', 64), ('mla_cq_g', 384), ('mla_ckv_g', 256), ('mla_q_g', 96),
                      ('mla_k_g', 96), ('diff_q_g', 64), ('diff_k_g', 64), ('diff_subln_g', 128)):
            gains[nm] = sbt(LS, nm, [128, n], F32)
            P.dma('sp', gains[nm][:], W[nm].partition_broadcast(128), [], [gains[nm].res], key='small')
        P.dma('sp', ident_f[:], C['ident'], [], [ident_f.res], key='small')
        O.copy('dve', ident_b[:], ident_f[:], [ident_f.res], [ident_b.res])
        O.memset('dve', ones_f[:], 1.0, [ones_f.res])
        O.memset('dve', eps_t[:], EPS, [eps_t.res])
        sg = gains['diff_subln_g']
        O.ts('dve', sg[:], sg[:], (1.0 - lam_init), ALU.mult, [sg.res], [sg.res])

        with ExitStack() as ph:
            wada = sbt(ph, 'wada', [128, 8, 3072], F32)
            wsrc = W['w_ada'].rearrange("(p k) n -> p k n", k=8)
            for i in range(4):
                P.dma('sp' if i % 2 == 0 else 'act', wada[:, 2 * i:2 * i + 2, :], wsrc[:, 2 * i:2 * i + 2, :], [],
                      [wada.res + str(i)], key='wada%d' % i)
            ccol = sbt(ph, 'ccol', [128, 2, 8], F32)
            P.dma('sp', ccol[:, 0, :], C['cvec'].rearrange("(p k) -> p k", k=8), [], [ccol.res + 'a'], key='small')
            P.dma('sp', ccol[:, 1, :], C['cctx'].rearrange("(p k) -> p k", k=8), [], [ccol.res + 'b'], key='small')
            sig = sbt(ph, 'sig', [128, 2, 8], F32)
            scol = sbt(ph, 'scol', [128, 2, 8], F32)
            O.act(sig[:], ccol[:], AF.Sigmoid, [ccol.res + 'a', ccol.res + 'b'], [sig.res])
            O.tt('dve', scol[:], ccol[:], sig[:], ALU.mult, [sig.res, ccol.res + 'a', ccol.res + 'b'], [scol.res])
            brow = sbt(ph, 'brow', [1, 3072], F32)
            P.dma('sp', brow[:], W['b_ada'].rearrange("(o n) -> o n", o=1), [], [brow.res], key='small')
            gnb = sbt(ph, 'gnb', [128, 1024], F32)
            P.dma('sp', gnb[:], W['norm_g'].partition_broadcast(128), [], [gnb.res], key='small')
            modrow = [sbt(ph, 'modrow%d' % i, [1, 3072], F32) for i in range(2)]
            pmod = pring(ph, 'pmod', 2, [128, 512], F32)
            for which in range(2 if DBG0 >= 2 else 0):
                for cg in range(6):
                    ps = pmod.next()
                    for k in range(8):
                        O.mm(ps[0:1, :], scol[:, which, k:k + 1], wada[:, k, cg * 512:(cg + 1) * 512], k == 0, k == 7,
                             [scol.res, wada.res + str(k // 2)], [ps.res])
                    O.tt('dve', modrow[which][0:1, cg * 512:(cg + 1) * 512], ps[0:1, :], brow[0:1, cg * 512:(cg + 1) * 512],
                         ALU.add, [ps.res, brow.res], [modrow[which].res])
            for which in range(2 if DBG0 >= 3 else 0):
                for cg in range(6):
                    ps = pmod.next()
                    O.mm(ps[:, :], ones_f[0:1, :], modrow[which][0:1, cg * 512:(cg + 1) * 512], True, True,
                         [ones_f.res, modrow[which].res], [ps.res])
                    c0 = (cg % 2) * 512
                    if cg < 2:
                        O.acopy(Sh_[which][:, c0:c0 + 512], ps[:, :], [ps.res], [Sh_[which].res])
                    elif cg < 4:
                        O.stt('dve', A_[which][:, c0:c0 + 512], ps[:, :], 1.0, gnb[:, c0:c0 + 512], ALU.add, ALU.mult,
                              [ps.res, gnb.res], [A_[which].res])
                    else:
                        O.acopy(Gt_[which][:, c0:c0 + 512], ps[:, :], [ps.res], [Gt_[which].res])
            if DBG0 < 4:
                P.barrier()
                return
            lrow = sbt(ph, 'lrow', [128, 4, 64], F32)
            P.dma('sp', lrow[:], W['diff_l'].rearrange("a d -> (a d)").partition_broadcast(128), [], [lrow.res], key='small')
            lprod = sbt(ph, 'lprod', [128, 2, 64], F32)
            lsum = sbt(ph, 'lsum', [128, 8], F32)
            O.tt('dve', lprod[:, 0, :], lrow[:, 0, :], lrow[:, 1, :], ALU.mult, [lrow.res], [lprod.res])
            O.tt('dve', lprod[:, 1, :], lrow[:, 2, :], lrow[:, 3, :], ALU.mult, [lrow.res, lprod.res], [lprod.res])
            O.red('dve', lsum[:, 0:2], lprod[:], [lprod.res], [lsum.res])
            O.act(lsum[:, 2:4], lsum[:, 0:2], AF.Exp, [lsum.res], [lsum.res])
            O.tt('dve', lsum[:, 4:5], lsum[:, 3:4], lsum[:, 2:3], ALU.subtract, [lsum.res], [lsum.res])
            O.ts('dve', nlam_b[:], lsum[:, 4:5], -lam_init, ALU.add, [lsum.res], [nlam_b.res])
        P.barrier()
        if DEBUG_STOP == 0 and DBG0 == 4:
            return
        if DEBUG_STOP == 0:
            for i, tl in enumerate([A_[0], Sh_[0], Gt_[0], A_[1], Sh_[1], Gt_[1]][:DBG0 - 4]):
                P.dma('sp', xout[i * 128:(i + 1) * 128, :], tl[:], [], [], key='dbg')
            P.barrier()
            return

        with ExitStack() as ph:
            NQKV = 3744
            wq = sbt(ph, 'wq', [128, 8, NQKV], BF16)
            wsrc = W['w_in'].rearrange("(k p) n -> p k n", p=128)
            for k in range(8):
                P.dma('pool', wq[:, k, :], wsrc[:, k, 0:NQKV], [], [wq.res + str(k)], key='wl%d' % k)
            wq_res = [wq.res + str(k) for k in range(8)]
            wuq = sbt(ph, 'wuq', [128, 3, 768], BF16)
            P.dma('pool', wuq[:], W['w_uq'].rearrange("(k p) n -> p k n", p=128), [], [wuq.res], key='wl0')
            wukv = sbt(ph, 'wukv', [128, 2, 1024], BF16)
            P.dma('pool', wukv[:], W['w_ukv'].rearrange("(k p) n -> p k n", p=128), [], [wukv.res], key='wl1')

            xt_r = sring(ph, 'xt', 2, [128, 1024], F32)
            rope_r = sring(ph, 'rope', 2, [128, 96], F32)
            junk = sbt(ph, 'junk', [128, 1024], F32)
            st_r = sring(ph, 'st', 2, [128, 4], F32)
            h1 = sbt(ph, 'h1', [128, 1024], F32)
            hb = sbt(ph, 'hb', [128, 1024], BF16)
            hT_r = sring(ph, 'hT', 2, [128, 8, 128], BF16)
            sq_r = sring(ph, 'sq', 2, [128, 768], F32)
            ss_r = sring(ph, 'ss', 3, [128, 24], F32)
            t_r = sring(ph, 't', 2, [128, 768], F32)
            tg_r = sring(ph, 'tg', 3, [128, 768], F32)
            dst_r = sring(ph, 'dst', 8, [128, 768], BF16)
            rtmp = [sbt(ph, 'rtmp%d' % i, [128, 256], F32) for i in range(4)]
            mkraw = sbt(ph, 'mkraw', [128, 8, 96], F32)
            krs_r = sring(ph, 'krs', 3, [128, 32], F32)
            carry = []
            cT_r = sring(ph, 'cT', 3, [128, 3, 128], BF16)
            stg4_r = sring(ph, 'stg4', 3, [128, 4, 128], BF16)
            stg8_r = sring(ph, 'stg8', 2, [96, 8, 128], BF16)
            vst_r = sring(ph, 'vst', 3, [128, 8, 66], BF16)
            vstd_r = sring(ph, 'vstd', 2, [128, 4, 132], BF16)
            for tl in vst_r.tiles + vstd_r.tiles:
                O.memset('pool', tl[:], 1.0, [tl.res])

            pT = pst(ph, 'pT', [128, 8, 128], BF16)
            pj_r = pring(ph, 'pj', 4, [128, 512], F32)
            ptq_r = pring(ph, 'ptq', 2, [128, 8, 128], BF16)

            def headnorm(src_ap, src_res, H, D, gain):
                n = H * D
                sq = sq_r.next(); ss = ss_r.next(); t = t_r.next(); tg = tg_r.next()
                O.act(sq[:, :n], src_ap, AF.Square, [src_res], [sq.res])
                O.red('dve', ss[:, 0:H], v3(sq[:, :n], H), [sq.res], [ss.res])
                O.act(ss[:, 8:8 + H], ss[:, 0:H], AF.Ln, [ss.res, eps_t.res], [ss.res], scale=1.0 / D, bias=eps_t[:])
                O.act(ss[:, 16:16 + H], ss[:, 8:8 + H], AF.Exp, [ss.res], [ss.res], scale=-0.5)
                O.tt('dve', v3(t[:, :n], H), v3(src_ap, H), ss[:, 16:16 + H].unsqueeze(2).to_broadcast([128, H, D]), ALU.mult,
                     [src_res, ss.res], [t.res])
                O.tt('pool', v3(tg[:, :n], H), v3(t[:, :n], H), gain[:, 0:D].unsqueeze(1).to_broadcast([128, H, D]), ALU.mult,
                     [t.res, gain.res], [tg.res])
                return tg

            def rope_into(t3, t_res, d3, d_res, H, r0, R, rope):
                if rope is None:
                    O.acopy(d3, t3, [t_res, d_res], [d_res])
                    return
                rt, co, so = rope
                hh = R // 2
                if r0 > 0:
                    O.acopy(d3[:, :, 0:r0], t3[:, :, 0:r0], [t_res, d_res], [d_res])
                x1 = t3[:, :, r0:r0 + hh]
                x2 = t3[:, :, r0 + hh:r0 + R]
                cs = rt[:, co:co + hh].unsqueeze(1).to_broadcast([128, H, hh])
                sn = rt[:, so:so + hh].unsqueeze(1).to_broadcast([128, H, hh])
                ra, rb, rc, rd = [v3(x[:, :H * hh], H) for x in rtmp]
                O.tt('dve', ra, x1, cs, ALU.mult, [t_res, rt.res], [rtmp[0].res])
                O.tt('pool', rb, x2, sn, ALU.mult, [t_res, rt.res], [rtmp[1].res])
                O.tt('pool', rc, x2, cs, ALU.mult, [t_res, rt.res], [rtmp[2].res])
                O.tt('dve', rd, x1, sn, ALU.mult, [t_res, rt.res], [rtmp[3].res])
                O.tt('dve', d3[:, :, r0:r0 + hh], ra, rb, ALU.subtract, [rtmp[0].res, rtmp[1].res, d_res], [d_res])
                O.tt('pool', d3[:, :, r0 + hh:r0 + R], rc, rd, ALU.add, [rtmp[2].res, rtmp[3].res, d_res], [d_res])

            def tstore4(dst, scr_ap, col0):
                pq = ptq_r.next()
                for c in range(4):
                    O.tr(pq[:, c, :], dst[:, c * 128:(c + 1) * 128], ident_b[:], [dst.res, ident_b.res], [pq.res])
                stg = stg4_r.next()
                O.acopy(stg[:], pq[:, 0:4, :], [pq.res], [stg.res])
                P.dma('sp', scr_ap.rearrange("(c p) n -> p c n", p=128)[:, :, col0:col0 + 128], stg[:], [stg.res], [], key=stg.res)

            def tstore8(dst, scr_ap, col0):
                pq = ptq_r.next()
                for c in range(8):
                    O.tr(pq[0:96, c, :], dst[:, c * 96:(c + 1) * 96], ident_b[:], [dst.res, ident_b.res], [pq.res])
                stg = stg8_r.next()
                O.acopy(stg[:], pq[0:96, :, :], [pq.res], [stg.res])
                P.dma('sp', scr_ap.rearrange("h p n -> p h n")[:, :, col0:col0 + 128], stg[:], [stg.res], [], key=stg.res)

            def proj(hT, c0, c1):
                ps = pj_r.next()
                n = c1 - c0
                for k in range(8):
                    O.mm(ps[:, 0:n], hT[:, k, :], wq[:, k, c0:c1], k == 0, k == 7, [hT.res, wq_res[k]], [ps.res])
                return ps

            def qk_simple(hT, c0, gain, rope, scr_ap, col0):
                ps = proj(hT, c0, c0 + 512)
                tg = headnorm(ps[:, :], ps.res, 8, 64, gain)
                dst = dst_r.next()
                rope_into(v3(tg[:, :512], 8), tg.res, v3(dst[:, :512], 8), dst.res, 8, 0, 64, rope)
                tstore4(dst, scr_ap, col0)

            def do_tile(src_ap, which, rope_src, full, kt, qt):
                xt = xt_r.next()
                P.dma('sp', xt[:], src_ap, [], [xt.res], key=xt.res)
                rp = None
                if rope_src is not None:
                    rp = rope_r.next()
                    P.dma('sp', rp[:], rope_src, [], [rp.res], key=rp.res)
                st = st_r.next()
                O.memset('dve', st[:], 0.0, [st.res])
                O.act(junk[:], xt[:], AF.Square, [xt.res, st.res], [junk.res, st.res], accum_out=st[:, 0:1])
                O.act(st[:, 1:2], st[:, 0:1], AF.Ln, [st.res, eps_t.res], [st.res], scale=1.0 / 1024, bias=eps_t[:])
                O.act(st[:, 2:3], st[:, 1:2], AF.Exp, [st.res], [st.res], scale=-0.5)
                O.stt('dve', h1[:], xt[:], st[:, 2:3], A_[which][:], ALU.mult, ALU.mult, [xt.res, st.res, A_[which].res], [h1.res])
                O.tt('pool', hb[:], h1[:], Sh_[which][:], ALU.add, [h1.res, Sh_[which].res], [hb.res])
                for k in range(8):
                    O.tr(pT[:, k, :], hb[:, k * 128:(k + 1) * 128], ident_b[:], [hb.res, ident_b.res], [pT.res])
                hT = hT_r.next()
                O.acopy(hT[:], pT[:], [pT.res], [hT.res])

                ropem = None if rp is None else (rp, 0, 16)
                roped = None if rp is None else (rp, 32, 64)

                def F_qk(c0, gain, rope):
                    ps = proj(hT, c0, c0 + 512)
                    tg = headnorm(ps[:, :], ps.res, 8, 64, gain)
                    dst = dst_r.next()
                    rope_into(v3(tg[:, :512], 8), tg.res, v3(dst[:, :512], 8), dst.res, 8, 0, 64, rope)
                    return dst

                def F_cq():
                    ps = proj(hT, 1536, 1920)
                    tg = headnorm(ps[:, 0:384], ps.res, 1, 384, gains['mla_cq_g'])
                    dst = dst_r.next()
                    O.acopy(dst[:, :384], tg[:, :384], [tg.res], [dst.res])
                    return dst

                def B_cq(dst):
                    pq = ptq_r.next()
                    for c in range(3):
                        O.tr(pq[:, c, :], dst[:, c * 128:(c + 1) * 128], ident_b[:], [dst.res, ident_b.res], [pq.res])
                    cT = cT_r.next()
                    O.acopy(cT[:], pq[:, 0:3, :], [pq.res], [cT.res])
                    dstq = dst_r.next()
                    for g in range(2):
                        ps = pj_r.next()
                        for k in range(3):
                            O.mm(ps[:, 0:384], cT[:, k, :], wuq[:, k, g * 384:(g + 1) * 384], k == 0, k == 2, [cT.res, wuq.res], [ps.res])
                        tg = headnorm(ps[:, 0:384], ps.res, 4, 96, gains['mla_q_g'])
                        rope_into(v3(tg[:, :384], 4), tg.res, v3(dstq[:, g * 384:(g + 1) * 384], 4), dstq.res, 4, 64, 32, ropem)
                    return dstq

                def F_ckv():
                    pskv = proj(hT, 1920, 2208)
                    tg = headnorm(pskv[:, 0:256], pskv.res, 1, 256, gains['mla_ckv_g'])
                    dst = dst_r.next()
                    O.acopy(dst[:, :256], tg[:, :256], [tg.res], [dst.res])
                    krs = krs_r.next()
                    O.acopy(krs[:], pskv[:, 256:288], [pskv.res], [krs.res])
                    return dst, krs

                def B_ckv(dst, krs):
                    pq = ptq_r.next()
                    for c in range(2):
                        O.tr(pq[:, c, :], dst[:, c * 128:(c + 1) * 128], ident_b[:], [dst.res, ident_b.res], [pq.res])
                    cT = cT_r.next()
                    O.acopy(cT[:, 0:2, :], pq[:, 0:2, :], [pq.res], [cT.res])
                    O.copy('pool', mkraw[:, :, 64:96], krs[:].unsqueeze(1).to_broadcast([128, 8, 32]), [krs.res, mkraw.res], [mkraw.res])
                    vs = vst_r.next()
                    for g in range(2):
                        ps = pj_r.next()
                        for k in range(2):
                            O.mm(ps[:, :], cT[:, k, :], wukv[:, k, g * 512:(g + 1) * 512], k == 0, k == 1, [cT.res, wukv.res], [ps.res])
                        p3 = v3(ps[:, :], 4)
                        O.acopy(mkraw[:, 4 * g:4 * g + 4, 0:64], p3[:, :, 0:64], [ps.res, mkraw.res], [mkraw.res])
                        O.acopy(vs[:, 4 * g:4 * g + 4, 0:64], p3[:, :, 64:128], [ps.res, vs.res], [vs.res])
                    P.dma('sp', scr['V_m'].rearrange("h p t d -> p h t d")[:, :, kt, :], vs[:], [vs.res], [], key=vs.res)
                    tg = headnorm(mkraw[:].rearrange("p h d -> p (h d)"), mkraw.res, 8, 96, gains['mla_k_g'])
                    dstk = dst_r.next()
                    rope_into(v3(tg[:, :768], 8), tg.res, v3(dstk[:, :768], 8), dstk.res, 8, 64, 32, ropem)
                    return dstk

                def run_carry():
                    while carry:
                        carry.pop(0)()

                d_naq = F_qk(0, gains['na_q_g'], None) if full else None
                d_nak = F_qk(512, gains['na_k_g'], None)
                run_carry()
                if full:
                    tstore4(d_naq, scr['QT_na'], qt * 128)
                ps = proj(hT, 1024, 1536)
                vs = vst_r.next()
                O.acopy(vs[:, :, 0:64], v3(ps[:, :], 8), [ps.res, vs.res], [vs.res])
                P.dma('sp', scr['V_na'].rearrange("h p t d -> p h t d")[:, :, kt, :], vs[:], [vs.res], [], key=vs.res)
                tstore4(d_nak, scr['KT_na'], kt * 128)
                d_cq = F_cq() if full else None
                d_ckv, krs = F_ckv()
                dstq = B_cq(d_cq) if full else None
                d_dq = F_qk(2208, gains['diff_q_g'], roped) if full else None
                dstk = B_ckv(d_ckv, krs)
                d_dk = F_qk(2720, gains['diff_k_g'], roped)
                if full:
                    tstore8(dstq, scr['QT_m'], qt * 128)
                ps = proj(hT, 3232, 3744)
                vsd = vstd_r.next()
                pd3 = v3(ps[:, :], 4)
                O.acopy(vsd[:, :, 0:64], pd3[:, :, 0:64], [ps.res, vsd.res], [vsd.res])
                O.acopy(vsd[:, :, 66:130], pd3[:, :, 64:128], [ps.res, vsd.res], [vsd.res])
                P.dma('sp', scr['V_d'].rearrange("h p t d -> p h t d")[:, :, kt, :], vsd[:], [vsd.res], [], key=vsd.res)
                if full:
                    carry.append(lambda d=d_dq, q=qt: tstore4(d, scr['QT_d'], q * 128))
                carry.append(lambda d=dstk, k_=kt: tstore8(d, scr['KT_m'], k_ * 128))
                carry.append(lambda d=d_dk, k_=kt: tstore4(d, scr['KT_d'], k_ * 128))

            for i in range(min(2, DBG1_TILES)):
                do_tile(ctxin[i * 128:(i + 1) * 128, :], 1, None, not last, i, i)
            for i in range(min(NLT, max(0, DBG1_TILES - 2))):
                do_tile(xown[i * 128:(i + 1) * 128, :], 0, C['rope_own'][i * 128:(i + 1) * 128, :], True, 2 + i, 2 + i)
            for i in range(min(NLT, max(0, DBG1_TILES - 34))):
                do_tile(xother[i * 128:(i + 1) * 128, :], 0, C['rope_other'][i * 128:(i + 1) * 128, :], both, 2 + NLT + i, 2 + NLT + i)
            while carry:
                carry.pop(0)()
        P.barrier()
        if DEBUG_STOP == 1:
            return

        with ExitStack() as ph:
            KT_r = sring(ph, 'KT', 2, [128, NK], BF16)
            QT_r = sring(ph, 'QT', 2, [128, NQL], BF16)
            V_r = sring(ph, 'V', 2, [128, NKT * 132], BF16)
            PT_r = sring(ph, 'PT', 4, [128, 1024], BF16)
            oT_r = sring(ph, 'oT', 2, [128, 512], F32)
            om_r = sring(ph, 'om', 2, [128, 4, 128], F32)
            ost_r = sring(ph, 'ost', 3, [128, 4, 128], BF16)
            rc_r = sring(ph, 'rc', 4, [128, 8], F32)
            dt_r = sring(ph, 'dt', 2, [128, 128], F32)
            dsq = sbt(ph, 'dsq', [128, 128], F32)
            NCH = 4
            TPC = (NTBL + NCH - 1) // NCH
            nabf_r = sring(ph, 'nab_f', 2, [128, TPC, 128], F32)
            nam_b = sbt(ph, 'nam_b', [128, NTBL, 128], BF16)
            nab_b = sbt(ph, 'nab_b', [128, NTBL, 128], BF16)
            P.dma('pool', nam_b[:], C['namask'], [], [nam_b.res], key='wl4')

            ps_s = pring(ph, 'ps_s', 3, [128, 512], F32)
            accT_r = pring(ph, 'accT', 2, [128, 512], F32)
            ps_o = pring(ph, 'ptr', 2, [128, 512], F32)

            def vview(V, dvp):
                return V[:, 0:NKT * (dvp + 1)].rearrange("p (t d) -> p t d", d=dvp + 1)

            def load_head(KT_src, QT_src, V_src, nrow, dvp):
                KT = KT_r.next(); QT = QT_r.next(); V = V_r.next()
                P.dma('sp', KT[0:nrow, :], KT_src, [], [KT.res], key=KT.res)
                P.dma('act', QT[0:nrow, 0:NQL], QT_src[:, 0:NQL], [], [QT.res], key=QT.res)
                P.dma('sp', vview(V, dvp), V_src, [], [V.res], key=V.res)
                return KT, QT, V

            def attend_T(KT, QT, V, p0, nrow, q0, nq, kts, scale, dvp, groups):
                Vv = vview(V, dvp)
                nqs = nq // 128
                nk = len(kts)
                accs = [accT_r.next() for _ in groups]
                pend = []

                def pvT(item):
                    idx, kt, pt = item
                    for gi, c0 in enumerate(groups):
                        O.mm(accs[gi][0:65, 0:nq], Vv[:, kt, c0:c0 + 65], pt[:, 0:nq], idx == 0, idx == nk - 1,
                             [pt.res, V.res], [accs[gi].res])
                for idx, kt in enumerate(kts):
                    ps = ps_s.next()
                    O.mm(ps[:, 0:nq], KT[p0:p0 + nrow, kt * 128:(kt + 1) * 128], QT[p0:p0 + nrow, q0:q0 + nq], True, True,
                         [KT.res, QT.res], [ps.res])
                    pt = PT_r.next()
                    O.act(pt[:, 0:nq], ps[:, 0:nq], AF.Exp, [ps.res], [pt.res], scale=scale)
                    pend.append((idx, kt, pt))
                    if len(pend) > 2:
                        pvT(pend.pop(0))
                while pend:
                    pvT(pend.pop(0))
                outs = []
                for gi in range(len(groups)):
                    oT = oT_r.next()
                    O.copy('dve', oT[0:65, 0:nq], accs[gi][0:65, 0:nq], [accs[gi].res], [oT.res])
                    ptr = ps_o.next()
                    for qs in range(nqs):
                        O.tr(ptr[:, qs * 128:qs * 128 + 65], oT[0:65, qs * 128:(qs + 1) * 128], ident_f[0:65, 0:65],
                             [oT.res, ident_f.res], [ptr.res])
                    outs.append(ptr)
                return outs

            def finalize_simple(accs, nqs, dv, ocol, q0):
                ost = ost_r.next()
                rc = rc_r.next()
                for qs in range(nqs):
                    acc, off = accs[qs]
                    O.recip(rc[:, qs:qs + 1], acc[:, off + dv:off + dv + 1], [acc.res, rc.res], [rc.res])
                    O.ts('dve', ost[:, qs, 0:dv], acc[:, off:off + dv], rc[:, qs:qs + 1], ALU.mult, [acc.res, rc.res, ost.res], [ost.res])
                P.dma('sp', scr['O'][q0:q0 + nqs * 128, :].rearrange("(s p) n -> p s n", p=128)[:, :, ocol:ocol + dv],
                      ost[:, 0:nqs, 0:dv], [ost.res], [], key=ost.res)

            all_kts = list(range(NKT))
            for pair in range(4 if DBG2 >= 1 else 0):
                KT = KT_r.next()
                P.dma('sp', KT[:, :], scr['KT_na'][pair * 128:(pair + 1) * 128, :], [], [KT.res], key=KT.res)
                QTz = []
                for hh in range(2):
                    q_ = QT_r.next()
                    P.dma('act', q_[:, 0:NQL], scr['QT_na'][pair * 128:(pair + 1) * 128, 0:NQL], [], [q_.res], key=q_.res)
                    O.memset('pool', q_[(1 - hh) * 64:(1 - hh) * 64 + 64, 0:NQL], 0.0, [q_.res])
                    QTz.append(q_)
                for hh in range(2):
                    h = pair * 2 + hh
                    QT = QTz[hh]
                    V = V_r.next()
                    Vv = vview(V, 65)
                    P.dma('sp', Vv, scr['V_na'][h], [], [V.res], key=V.res)
                    for c_ in range(NCH):
                        t0_, t1_ = c_ * TPC, min(NTBL, (c_ + 1) * TPC)
                        nf = nabf_r.next()
                        P.dma('sp', nf[:, 0:t1_ - t0_, :], W['nab'][h][:, t0_:t1_, :], [], [nf.res], key=nf.res)
                        O.stt('dve', nab_b[:, t0_:t1_, :], nf[:, 0:t1_ - t0_, :], 1.0 / NA_SCALE, nam_b[:, t0_:t1_, :], ALU.mult, ALU.add,
                              [nf.res, nam_b.res, nab_b.res], [nab_b.res])
                    if not last:
                        (ptr,) = attend_T(KT, QT, V, 0, 128, 0, 256, [0, 1], NA_SCALE, 65, [0])
                        finalize_simple([(ptr, 0), (ptr, 128)], 2, 64, h * 64, 0)
                    qtiles = [('own', i_) for i_ in range(NLT)] + ([('other', i_) for i_ in range(NLT)] if both else [])
                    for grp in range(len(qtiles) // 4):
                        acc = ps_o.next()
                        accs = [(acc, 128 * j) for j in range(4)]
                        for j in range(4):
                            kind, li = qtiles[grp * 4 + j]
                            b_self, b_cross, tb0 = (2, 2 + NLT, 5) if kind == 'own' else (2 + NLT, 2, 29)
                            if 2 <= li <= NLT - 3:
                                slots = [(b_self + li + d, d + 2) for d in range(-2, 3)]
                            elif li < 2:
                                slots = [(b_self + s, tb0 + li * 6 + s) for s in range(4)] + \
                                        [(b_cross + NLT - 2 + s, tb0 + li * 6 + 4 + s) for s in range(2)]
                            else:
                                e_ = li - (NLT - 2) + 2
                                slots = [(b_self + NLT - 4 + s, tb0 + e_ * 6 + s) for s in range(4)] + \
                                        [(b_cross + s, tb0 + e_ * 6 + 4 + s) for s in range(2)]
                            qti = li if kind == 'own' else NLT + li
                            q0 = CTX + qti * 128
                            kts = [0, 1] + [s_[0] for s_ in slots]
                            nsl = len(kts)
                            pss = [ps_s.next(), ps_s.next()]
                            for si, kt in enumerate(kts):
                                ps = pss[si // 4]
                                c0 = (si % 4) * 128
                                tb = None if si < 2 else slots[si - 2][1]
                                O.mm(ps[:, c0:c0 + 128], KT[0:128, kt * 128:(kt + 1) * 128], QT[0:128, q0:q0 + 128],
                                     True, tb is None, [KT.res, QT.res], [ps.res])
                                if tb is not None:
                                    O.mm(ps[:, c0:c0 + 128], ident_b[:], nab_b[:, tb, :], False, True, [ident_b.res, nab_b.res], [ps.res])
                            pt = PT_r.next()
                            O.act(pt[:, 0:512], pss[0][:, :], AF.Exp, [pss[0].res], [pt.res], scale=NA_SCALE)
                            nb = (nsl - 4) * 128
                            O.act(pt[:, 512:512 + nb], pss[1][:, 0:nb], AF.Exp, [pss[1].res, pt.res], [pt.res], scale=NA_SCALE)
                            for si, kt in enumerate(kts):
                                O.mm(acc[:, 128 * j:128 * j + 65], pt[:, si * 128:(si + 1) * 128], Vv[:, kt, 0:65], si == 0, si == nsl - 1,
                                     [pt.res, V.res], [acc.res])
                        finalize_simple(accs, 4, 64, h * 64, CTX + grp * 512)
            for h in range(8 if DBG2 >= 2 else 0):
                KT, QT, V = load_head(scr['KT_m'][h], scr['QT_m'][h], scr['V_m'][h], 96, 65)
                if not last:
                    (ptr,) = attend_T(KT, QT, V, 0, 96, 0, 256, [0, 1], MLA_SCALE, 65, [0])
                    finalize_simple([(ptr, 0), (ptr, 128)], 2, 64, 512 + h * 64, 0)
                for ch in range((NQL - CTX) // 512):
                    (ptr,) = attend_T(KT, QT, V, 0, 96, CTX + ch * 512, 512, all_kts, MLA_SCALE, 65, [0])
                    finalize_simple([(ptr, 128 * j) for j in range(4)], 4, 64, 512 + h * 64, CTX + ch * 512)

            def finalize_diff(om0, om1, nqs, h, q0):
                ost = ost_r.next()
                sgb = gains['diff_subln_g']
                for qs in range(nqs):
                    dt = dt_r.next()
                    rc = rc_r.next()
                    O.stt('dve', dt[:], om1[:, qs, :], nlam_b[:], om0[:, qs, :], ALU.mult, ALU.add, [om0.res, om1.res, nlam_b.res], [dt.res])
                    O.tt('dve', dsq[:], dt[:], dt[:], ALU.mult, [dt.res], [dsq.res])
                    O.red('dve', rc[:, 3:4], dsq[:], [dsq.res, rc.res], [rc.res])
                    O.act(rc[:, 4:5], rc[:, 3:4], AF.Ln, [rc.res, eps_t.res], [rc.res], scale=1.0 / 128, bias=eps_t[:])
                    O.act(rc[:, 5:6], rc[:, 4:5], AF.Exp, [rc.res], [rc.res], scale=-0.5)
                    O.stt('dve', ost[:, qs, :], dt[:], rc[:, 5:6], sgb[:], ALU.mult, ALU.mult, [dt.res, rc.res, sgb.res, ost.res], [ost.res])
                P.dma('sp', scr['O'][q0:q0 + nqs * 128, :].rearrange("(s p) n -> p s n", p=128)[:, :, 1024 + h * 128:1024 + (h + 1) * 128],
                      ost[:, 0:nqs, :], [ost.res], [], key=ost.res)

            for h in range(4 if DBG2 >= 3 else 0):
                KT = KT_r.next(); V = V_r.next()
                P.dma('sp', KT[:, :], scr['KT_d'][h * 128:(h + 1) * 128, :], [], [KT.res], key=KT.res)
                P.dma('sp', vview(V, 131), scr['V_d'][h], [], [V.res], key=V.res)
                QTm = []
                for m in range(2):
                    QTz = QT_r.next()
                    P.dma('act', QTz[:, 0:NQL], scr['QT_d'][h * 128:(h + 1) * 128, 0:NQL], [], [QTz.res], key=QTz.res)
                    O.memset('pool', QTz[(1 - m) * 64:(1 - m) * 64 + 64, 0:NQL], 0.0, [QTz.res])
                    QTm.append(QTz)
                chunks = []
                if not last:
                    chunks.append((0, 256, [0, 1]))
                for ch in range((NQL - CTX) // 512):
                    chunks.append((CTX + ch * 512, 512, all_kts))
                for (q0, nq, kts) in chunks:
                    nqs = nq // 128
                    oms = []
                    for m in range(2):
                        ptrs = attend_T(KT, QTm[m], V, 0, 128, q0, nq, kts, DIFF_SCALE, 131, [0, 66])
                        om = om_r.next()
                        for gi, ptr in enumerate(ptrs):
                            rc = rc_r.next()
                            for qs in range(nqs):
                                O.recip(rc[:, qs:qs + 1], ptr[:, qs * 128 + 64:qs * 128 + 65], [ptr.res, rc.res], [rc.res])
                                O.ts('dve', om[:, qs, gi * 64:(gi + 1) * 64], ptr[:, qs * 128:qs * 128 + 64], rc[:, qs:qs + 1], ALU.mult,
                                     [ptr.res, rc.res, om.res], [om.res])
                        oms.append(om)
                    finalize_diff(oms[0], oms[1], nqs, h, q0)
        P.barrier()
        if DEBUG_STOP == 2:
            return

        with ExitStack() as ph:
            NZ = 4608
            wz = sbt(ph, 'wz', [128, 8, NZ], BF16)
            wsrc = W['w_in'].rearrange("(k p) n -> p k n", p=128)
            for k in range(8):
                P.dma('pool', wz[:, k, :], wsrc[:, k, 3744:D_IN], [], [wz.res + str(k)], key='wl%d' % k)
            wz_res = [wz.res + str(k) for k in range(8)]
            wbr = sbt(ph, 'wbr', [128, 12, 1024], BF16)
            for b_ in range(3):
                P.dma('pool', wbr[:, 4 * b_:4 * b_ + 4, :], W['w_br'][b_].rearrange("(k p) n -> p k n", p=128), [], [wbr.res + str(b_)],
                      key='wl%d' % b_)
            wout = sbt(ph, 'wout', [128, 8, 1024], BF16)
            P.dma('pool', wout[:], W['w_out'].rearrange("(k p) n -> p k n", p=128), [], [wout.res], key='wl3')

            xt_r = sring(ph, 'xt3', 2, [128, 1024], F32)
            o_r = sring(ph, 'o3', 2, [128, 1536], BF16)
            st_r = sring(ph, 'st3', 2, [128, 4], F32)
            et_r = sring(ph, 'et', 2, [128, 512], F32)
            hb = sbt(ph, 'hb3', [128, 1024], BF16)
            hT = sbt(ph, 'hT3', [128, 8, 128], BF16)
            zs = sbt(ph, 'zs', [128, 1536], F32)
            zg = sbt(ph, 'zg', [128, 1536], BF16)
            sgm = sbt(ph, 'sgm', [128, 3072], BF16)
            og = sbt(ph, 'og', [128, 1536], BF16)
            ogT = sbt(ph, 'ogT', [128, 12, 128], BF16)
            yacc = sbt(ph, 'yacc', [128, 1024], F32)
            ytmp = sbt(ph, 'ytmp', [128, 1024], F32)
            yb = sbt(ph, 'yb', [128, 1024], BF16)
            yT = sbt(ph, 'yT', [128, 8, 128], BF16)
            pT = pst(ph, 'pT3', [128, 8, 128], BF16)
            pT2 = pst(ph, 'pT23', [128, 8, 128], BF16)
            pj_r = pring(ph, 'pj3', 4, [128, 512], F32)

            def out_tile(src_ap, which, qrow0, dst_ap):
                xt = xt_r.next()
                P.dma('sp', xt[:], src_ap, [], [xt.res], key=xt.res)
                ot = o_r.next()
                P.dma('act', ot[:], scr['O'][qrow0:qrow0 + 128, :], [], [ot.res], key=ot.res)
                st = st_r.next()
                O.memset('dve', st[:], 0.0, [st.res])
                O.act(yacc[:], xt[:], AF.Square, [xt.res, st.res], [yacc.res, st.res], accum_out=st[:, 0:1])
                O.act(st[:, 1:2], st[:, 0:1], AF.Ln, [st.res, eps_t.res], [st.res], scale=1.0 / 1024, bias=eps_t[:])
                O.act(st[:, 2:3], st[:, 1:2], AF.Exp, [st.res], [st.res], scale=-0.5)
                O.stt('dve', ytmp[:], xt[:], st[:, 2:3], A_[which][:], ALU.mult, ALU.mult, [xt.res, st.res, A_[which].res], [ytmp.res])
                O.tt('pool', hb[:], ytmp[:], Sh_[which][:], ALU.add, [ytmp.res, Sh_[which].res], [hb.res])
                for k in range(8):
                    O.tr(pT[:, k, :], hb[:, k * 128:(k + 1) * 128], ident_b[:], [hb.res, ident_b.res], [pT.res])
                O.acopy(hT[:], pT[:], [pT.res], [hT.res])
                if DBG3 < 2:
                    return
                for g in range(9):
                    ps = pj_r.next()
                    for k in range(8):
                        O.mm(ps[:, :], hT[:, k, :], wz[:, k, g * 512:(g + 1) * 512], k == 0, k == 7, [hT.res, wz_res[k]], [ps.res])
                    et = et_r.next()
                    O.act(et[:], ps[:, :], AF.Exp, [ps.res], [et.res], scale=-1.0)
                    O.ts('dve', et[:], et[:], 1.0, ALU.add, [et.res], [et.res])
                    if g < 3:
                        O.recip(zg[:, g * 512:(g + 1) * 512], et[:], [et.res, zg.res], [zg.res])
                        O.copy('dve', zs[:, g * 512:(g + 1) * 512], ps[:, :], [ps.res, zs.res], [zs.res])
                    else:
                        O.recip(sgm[:, (g - 3) * 512:(g - 2) * 512], et[:], [et.res, sgm.res], [sgm.res])
                if DBG3 < 3:
                    return
                O.tt('pool', zs[:], zs[:], zg[:], ALU.mult, [zs.res, zg.res], [zs.res])
                O.tt('dve', og[:], zs[:], ot[:], ALU.mult, [zs.res, ot.res], [og.res])
                if DBG3 < 4:
                    return
                for c in range(12):
                    pp = pT if c < 8 else pT2
                    O.tr(pp[:, c % 8, :], og[:, c * 128:(c + 1) * 128], ident_b[:], [og.res, ident_b.res], [pp.res])
                O.acopy(ogT[:, 0:8, :], pT[:], [pT.res, ogT.res], [ogT.res])
                O.copy('dve', ogT[:, 8:12, :], pT2[:, 0:4, :], [pT2.res, ogT.res], [ogT.res])
                if DBG3 < 5:
                    return
                for b_ in range(3):
                    for g in range(2):
                        ps = pj_r.next()
                        for k in range(4):
                            O.mm(ps[:, :], ogT[:, 4 * b_ + k, :], wbr[:, 4 * b_ + k, g * 512:(g + 1) * 512], k == 0, k == 3,
                                 [ogT.res, wbr.res + str(b_)], [ps.res])
                        gsl = sgm[:, b_ * 1024 + g * 512:b_ * 1024 + (g + 1) * 512]
                        ysl = yacc[:, g * 512:(g + 1) * 512]
                        tsl = ytmp[:, g * 512:(g + 1) * 512]
                        if b_ == 0:
                            O.tt('dve', ysl, ps[:, :], gsl, ALU.mult, [ps.res, sgm.res, yacc.res], [yacc.res])
                        else:
                            O.tt('dve', tsl, ps[:, :], gsl, ALU.mult, [ps.res, sgm.res, ytmp.res], [ytmp.res])
                            if b_ == 1:
                                O.tt('pool', ysl, ysl, tsl, ALU.add, [ytmp.res, yacc.res], [yacc.res])
                            else:
                                O.tt('pool', yb[:, g * 512:(g + 1) * 512], ysl, tsl, ALU.add, [ytmp.res, yacc.res, yb.res], [yb.res])
                if DBG3 < 6:
                    return
                for k in range(8):
                    O.tr(pT[:, k, :], yb[:, k * 128:(k + 1) * 128], ident_b[:], [yb.res, ident_b.res], [pT.res])
                O.acopy(yT[:], pT[:], [pT.res], [yT.res])
                for g in range(2):
                    ps = pj_r.next()
                    for k in range(8):
                        O.mm(ps[:, :], yT[:, k, :], wout[:, k, g * 512:(g + 1) * 512], k == 0, k == 7, [yT.res, wout.res], [ps.res])
                    O.tt('dve', ytmp[:, g * 512:(g + 1) * 512], ps[:, :], Gt_[which][:, g * 512:(g + 1) * 512], ALU.mult,
                         [ps.res, Gt_[which].res, ytmp.res], [ytmp.res])
                if DBG3 < 7:
                    return
                O.tt('pool', xt[:], xt[:], ytmp[:], ALU.add, [xt.res, ytmp.res], [xt.res])
                P.dma('sp', dst_ap, xt[:], [xt.res], [], key=xt.res + 'o')

            if not last:
                for i in range(min(2, DBG3_TILES)):
                    out_tile(ctxin[i * 128:(i + 1) * 128, :], 1, i * 128, ctxout[i * 128:(i + 1) * 128, :])
            for i in range(min(NLT, max(0, DBG3_TILES - 2))):
                out_tile(xown[i * 128:(i + 1) * 128, :], 0, CTX + i * 128, xout[i * 128:(i + 1) * 128, :])
            if both:
                for i in range(NLT):
                    out_tile(xother[i * 128:(i + 1) * 128, :], 0, CTX + (NLT + i) * 128, xout_other[i * 128:(i + 1) * 128, :])
        P.barrier()


def declare_scratch(nc, sfx):
    s = {}
    def d(name, shape):
        s[name] = nc.dram_tensor('scr_' + name + sfx, shape, BF16, kind=("ExternalOutput" if DEBUG_SCR else "Internal")).ap()
    d('QT_na', [512, NQ2]); d('KT_na', [512, NK]); d('V_na', [8, 128, NKT, 66])
    d('QT_m', [8, 96, NQ2]); d('KT_m', [8, 96, NK]); d('V_m', [8, 128, NKT, 66])
    d('QT_d', [512, NQ2]); d('KT_d', [512, NK]); d('V_d', [4, 128, NKT, 132])
    d('O', [NQ2, 1536])
    return s


def build_fused():
    nc = bass.Bass("TRN2", target_bir_lowering=False)
    C = {}
    for name, shape in (('cvec', [1024]), ('cctx', [1024]), ('rope_own', [HALF, 96]), ('rope_other', [HALF, 96]),
                        ('namask', [128, NTBL, 128]), ('ident', [128, 128])):
        C[name] = nc.dram_tensor(name, shape, F32, kind="ExternalInput").ap()
    xown = nc.dram_tensor('xown', [HALF, 1024], F32, kind="ExternalInput").ap()
    xother = nc.dram_tensor('xother', [HALF, 1024], F32, kind="ExternalInput").ap()
    ctxin = nc.dram_tensor('ctxin', [CTX, 1024], F32, kind="ExternalInput").ap()
    xout = nc.dram_tensor('xout', [HALF, 1024], F32, kind="ExternalOutput").ap()
    x1own = nc.dram_tensor('x1own', [HALF, 1024], F32, kind="Internal").ap()
    x1other = nc.dram_tensor('x1other', [HALF, 1024], F32, kind="Internal").ap()
    ctx1 = nc.dram_tensor('ctx1', [CTX, 1024], F32, kind="Internal").ap()
    W0 = declare_layer_weights(nc, '_0')
    W1 = declare_layer_weights(nc, '_1')
    scr = declare_scratch(nc, '')
    P = Prog(nc)
    with ExitStack() as st:
        emit_layer(nc, P, 0, False, W0, C, xown, xother, ctxin, x1own, ctx1, scr, both=True, xout_other=x1other)
        emit_layer(nc, P, 1, True, W1, C, x1own, x1other, ctx1, xout, None, scr)
        P.emit(st)
    return nc, P


def rope_table():
    t = np.arange(SEQ)
    row = (t // GRID_W).astype(np.float32)
    col = (t % GRID_W).astype(np.float32)
    out = np.zeros((SEQ, 96), np.float32)
    for rot, off in ((32, 0), (64, 32)):
        nf = rot // 4
        inv = (10000.0 ** (-np.arange(nf, dtype=np.float32) / nf)).astype(np.float32)
        ang = np.concatenate([row[:, None] * inv, col[:, None] * inv], axis=-1).astype(np.float32)
        hh = rot // 2
        out[:, off:off + hh] = np.cos(ang)
        out[:, off + hh:off + 2 * hh] = np.sin(ang)
    return out


def na_tables(half):
    def table(i, j):
        kr = np.arange(128) // 64; kc = np.arange(128) % 64
        qr = np.arange(128) // 64; qc = np.arange(128) % 64
        r = (2 * i + qr)[None, :]
        krow = (2 * j + kr)[:, None]
        rs = np.clip(r - 4, 0, 120)
        vrow = (krow >= rs) & (krow <= rs + 7)
        cstart = np.clip(qc - 8, 0, 48)[None, :]
        vcol = (kc[:, None] >= cstart) & (kc[:, None] < cstart + 16)
        dr = np.clip(krow - r + 7, 0, 14)
        dc = np.clip(kc[:, None] - qc[None, :], -15, 15) + 15
        valid = vrow & vcol & (j >= 0) & (j < 64)
        return np.broadcast_to(dr, (128, 128)), dc, valid
    tabs = []
    for d in range(-2, 3):
        tabs.append(table(10, 10 + d))
    own_g = lambda li: half * NLT + li
    oth_g = lambda oi: (1 - half) * NLT + oi
    for self_g, cross_g in ((own_g, oth_g), (oth_g, own_g)):
        for li in (0, 1, NLT - 2, NLT - 1):
            i = self_g(li)
            if li < 2:
                keys = [self_g(s_) for s_ in range(4)] + [cross_g(NLT - 2 + s_) for s_ in range(2)]
            else:
                keys = [self_g(NLT - 4 + s_) for s_ in range(4)] + [cross_g(s_) for s_ in range(2)]
            for j in keys:
                tabs.append(table(i, j))
    dr = np.stack([t[0] for t in tabs]); dc = np.stack([t[1] for t in tabs]); va = np.stack([t[2] for t in tabs])
    return dr, dc, va


_CACHE = {}


def _get_prog():
    if 'f' not in _CACHE:
        _CACHE['f'] = build_fused()
    return _CACHE['f'][0]


def kernel(x, c, ctx, c_ctx, norm_g, w_ada, b_ada, w_in, na_rpb, na_q_g, na_k_g, mla_cq_g, mla_ckv_g, w_uq, w_ukv,
           mla_q_g, mla_k_g, diff_q_g, diff_k_g, diff_lq1, diff_lk1, diff_lq2, diff_lk2, diff_subln_g, w_br, w_out):
    f = lambda a: np.ascontiguousarray(np.asarray(a, dtype=np.float32))
    x = f(x); c = f(c); ctx = f(ctx); c_ctx = f(c_ctx)
    na_rpb = f(na_rpb)
    rt = rope_table()
    ident = np.eye(128, dtype=np.float32)
    wmaps = {}
    for l in range(DEPTH):
        p = {
            'norm_g': f(norm_g[l]), 'w_ada': f(w_ada[l]), 'b_ada': f(b_ada[l]), 'w_in': f(w_in[l]),
            'na_q_g': f(na_q_g[l]), 'na_k_g': f(na_k_g[l]), 'mla_cq_g': f(mla_cq_g[l]), 'mla_ckv_g': f(mla_ckv_g[l]),
            'w_uq': f(w_uq[l]), 'w_ukv': f(w_ukv[l]), 'mla_q_g': f(mla_q_g[l]), 'mla_k_g': f(mla_k_g[l]),
            'diff_q_g': f(diff_q_g[l]), 'diff_k_g': f(diff_k_g[l]),
            'diff_l': np.ascontiguousarray(np.stack([f(diff_lq1[l]), f(diff_lk1[l]), f(diff_lq2[l]), f(diff_lk2[l])])),
            'diff_subln_g': f(diff_subln_g[l]), 'w_br': f(w_br[l]), 'w_out': f(w_out[l]),
        }
        for k_, v in p.items():
            wmaps['%s_%d' % (k_, l)] = v
    per_half = []
    for half in range(2):
        dr, dc, va = na_tables(half)
        mask = np.where(va, 0.0, MASKVAL).astype(np.float32)
        d = {'rope_own': np.ascontiguousarray(rt[half * HALF:(half + 1) * HALF]),
             'rope_other': np.ascontiguousarray(rt[(1 - half) * HALF:(2 - half) * HALF]),
             'namask': np.ascontiguousarray(mask.transpose(1, 0, 2))}
        for l in range(DEPTH):
            g = na_rpb[l][:, dr, dc]
            g = np.where(va[None], g, np.float32(0.0))
            d['nab_%d' % l] = np.ascontiguousarray(g.transpose(0, 2, 1, 3))
        per_half.append(d)
    maps = []
    for core in range(8):
        b, half = core // 2, core % 2
        m = {'xown': np.ascontiguousarray(x[b, half * HALF:(half + 1) * HALF]),
             'xother': np.ascontiguousarray(x[b, (1 - half) * HALF:(2 - half) * HALF]),
             'ctxin': np.ascontiguousarray(ctx[b]), 'cvec': np.ascontiguousarray(c[b]), 'cctx': c_ctx, 'ident': ident}
        m.update(per_half[half])
        m.update(wmaps)
        maps.append(m)
    nc = _get_prog()
    res = run_bass_kernel_spmd(nc, maps, core_ids=list(range(8)))
    out = np.empty_like(x)
    for core in range(8):
        b, half = core // 2, core % 2
        out[b, half * HALF:(half + 1) * HALF] = res.results[core]['xout']
    return out
```
